# Optimizing a Trainium2 kernel written in Bass

```python
import jax, jax.numpy as jnp
from jax import lax
import numpy as np

D_MODEL = 2048
BATCH = 2
SEQ = 16384
DEPTH = 2

GRID_W = 64
CTX_LEN = 256
HEAD_DIM = 128
N_HEAD_SLOTS = D_MODEL // HEAD_DIM
NA_HEADS = N_HEAD_SLOTS // 4
NA_KH = 8
NA_KW = 16
GLA_HEADS = N_HEAD_SLOTS // 4
GLA_DV = HEAD_DIM
GLA_DK = HEAD_DIM // 2
GLA_GATE_RANK = 16
GLA_GATE_NORM = 16.0
GLA_CHUNK = 64
SWA_Q_HEADS = N_HEAD_SLOTS // 2
SWA_KV_HEADS = SWA_Q_HEADS // 4
SWA_WINDOW = 128
SWA_BLOCK = 128
ROPE_THETA = 10000.0
ROPE_AXIS_DIM = HEAD_DIM // 2
NA_W = NA_HEADS * HEAD_DIM
GLA_QK_W = GLA_HEADS * GLA_DK
GLA_V_W = GLA_HEADS * GLA_DV
SWA_Q_W = SWA_Q_HEADS * HEAD_DIM
SWA_KV_W = SWA_KV_HEADS * HEAD_DIM
D_MIX = NA_W + GLA_V_W + SWA_Q_W
IN_SPLITS = (NA_W, NA_W, NA_W,
             GLA_QK_W, GLA_QK_W, GLA_V_W, GLA_V_W, GLA_GATE_RANK, GLA_GATE_RANK,
             SWA_Q_W, SWA_KV_W, SWA_KV_W)
D_IN = sum(IN_SPLITS)
D_FF = 4 * D_MODEL
EPS = 1e-6
NEG_INF = -1e30

kernel_name = "hybrid_natten_gla_swa_dit_block"


def _rmsnorm(x, g):
    xf = x.astype(jnp.float32)
    y = xf * lax.rsqrt(jnp.mean(xf * xf, axis=-1, keepdims=True) + EPS)
    return (y * g.astype(jnp.float32)).astype(x.dtype)


def _heads(t, h):
    return t.reshape(t.shape[:-1] + (h, t.shape[-1] // h))


def _split_cols(z):
    idx = [int(v) for v in np.cumsum(IN_SPLITS)[:-1]]
    return jnp.split(z, idx, axis=-1)


def _axial_rope(L):
    t = jnp.arange(L, dtype=jnp.int32)
    row = (t // GRID_W).astype(jnp.float32)
    col = (t % GRID_W).astype(jnp.float32)
    n_freq = ROPE_AXIS_DIM // 2
    inv = ROPE_THETA ** (-jnp.arange(n_freq, dtype=jnp.float32) / n_freq)
    ar = row[:, None] * inv[None]
    ac = col[:, None] * inv[None]
    return (jnp.cos(ar)[:, None, :], jnp.sin(ar)[:, None, :],
            jnp.cos(ac)[:, None, :], jnp.sin(ac)[:, None, :])


def _apply_rope(x, tabs):
    cr, sr, cc, sc = tabs
    xf = x.astype(jnp.float32)
    xr, xcol = jnp.split(xf, 2, axis=-1)

    def rot(u, cos, sin):
        u1, u2 = jnp.split(u, 2, axis=-1)
        return jnp.concatenate([u1 * cos - u2 * sin, u2 * cos + u1 * sin], axis=-1)

    return jnp.concatenate([rot(xr, cr, sr), rot(xcol, cc, sc)], axis=-1).astype(x.dtype)


def _dense_attention(q, k, v, sink):
    B, M, Hq, hd = q.shape
    Hkv = k.shape[2]
    G = Hq // Hkv
    qg = q.reshape(B, M, Hkv, G, hd)
    s = jnp.einsum('bqhgd,bkhd->bhgqk', qg, k, preferred_element_type=jnp.float32) * (hd ** -0.5)
    if sink is not None:
        snk = jnp.broadcast_to(sink.astype(jnp.float32).reshape(1, Hkv, G, 1, 1), s.shape[:-1] + (1,))
        s = jnp.concatenate([s, snk], axis=-1)
    p = jax.nn.softmax(s, axis=-1)
    if sink is not None:
        p = p[..., :-1]
    o = jnp.einsum('bhgqk,bkhd->bqhgd', p.astype(v.dtype), v)
    return o.reshape(B, M, Hq, hd)


def _neighbourhood_attention(q, k, v, kc, vc, rpb):
    B, L, H, hd = q.shape
    rows = L // GRID_W
    kh = min(NA_KH, rows)
    scale = hd ** -0.5
    qg = q.reshape(B, rows, GRID_W, H, hd).transpose(1, 0, 2, 3, 4)
    kg = k.reshape(B, rows, GRID_W, H, hd)
    vg = v.reshape(B, rows, GRID_W, H, hd)
    cols = np.arange(GRID_W)
    col_start = np.clip(cols - NA_KW // 2, 0, GRID_W - NA_KW)
    col_idx = col_start[:, None] + np.arange(NA_KW)[None, :]
    col_bias_idx = col_idx - cols[:, None] + NA_KW - 1
    n_loc = kh * NA_KW

    def one_row(args):
        r, q_row = args
        rs = jnp.clip(r - kh // 2, 0, rows - kh)
        k_rows = lax.dynamic_slice_in_dim(kg, rs, kh, axis=1)
        v_rows = lax.dynamic_slice_in_dim(vg, rs, kh, axis=1)
        k_win = k_rows[:, :, col_idx]
        v_win = v_rows[:, :, col_idx]
        s_loc = jnp.einsum('bchd,bicjhd->bhcij', q_row, k_win, preferred_element_type=jnp.float32) * scale
        row_bias_idx = rs + jnp.arange(kh) - r + NA_KH - 1
        bias = rpb[:, row_bias_idx[None, :, None], col_bias_idx[:, None, :]]
        s_loc = (s_loc + bias[None].astype(jnp.float32)).reshape(B, H, GRID_W, n_loc)
        s_ctx = jnp.einsum('bchd,bmhd->bhcm', q_row, kc, preferred_element_type=jnp.float32) * scale
        p = jax.nn.softmax(jnp.concatenate([s_loc, s_ctx], axis=-1), axis=-1).astype(v.dtype)
        p_loc = p[..., :n_loc].reshape(B, H, GRID_W, kh, NA_KW)
        p_ctx = p[..., n_loc:]
        return (jnp.einsum('bhcij,bicjhd->bchd', p_loc, v_win)
                + jnp.einsum('bhcm,bmhd->bchd', p_ctx, vc))

    out = lax.map(one_row, (jnp.arange(rows), qg))
    return out.transpose(1, 0, 2, 3, 4).reshape(B, L, H, hd)


def _window_attention(q, k, v, kc, vc, sink):
    B, L, Hq, hd = q.shape
    Hkv = k.shape[2]
    G = Hq // Hkv
    nb = L // SWA_BLOCK
    band = 3 * SWA_BLOCK
    scale = hd ** -0.5
    pad = ((0, 0), (SWA_BLOCK, SWA_BLOCK), (0, 0), (0, 0))
    kp = jnp.pad(k, pad)
    vp = jnp.pad(v, pad)
    qb = q.reshape(B, nb, SWA_BLOCK, Hkv, G, hd).transpose(1, 0, 2, 3, 4, 5)
    rel = jnp.arange(band)[None, :] - SWA_BLOCK - jnp.arange(SWA_BLOCK)[:, None]
    in_window = jnp.abs(rel) <= SWA_WINDOW
    sink_logit = sink.astype(jnp.float32).reshape(1, Hkv, G, 1, 1)

    def one_block(args):
        n, q_blk = args
        start = n * SWA_BLOCK
        k_blk = lax.dynamic_slice_in_dim(kp, start, band, axis=1)
        v_blk = lax.dynamic_slice_in_dim(vp, start, band, axis=1)
        kpos = start - SWA_BLOCK + jnp.arange(band)
        valid = in_window & ((kpos >= 0) & (kpos < L))[None, :]
        s_loc = jnp.einsum('bqhgd,bkhd->bhgqk', q_blk, k_blk, preferred_element_type=jnp.float32) * scale
        s_loc = jnp.where(valid, s_loc, NEG_INF)
        s_ctx = jnp.einsum('bqhgd,bmhd->bhgqm', q_blk, kc, preferred_element_type=jnp.float32) * scale
        s_snk = jnp.broadcast_to(sink_logit, (B, Hkv, G, SWA_BLOCK, 1))
        p = jax.nn.softmax(jnp.concatenate([s_loc, s_ctx, s_snk], axis=-1), axis=-1).astype(v.dtype)
        p_loc = p[..., :band]
        p_ctx = p[..., band:band + kc.shape[1]]
        return (jnp.einsum('bhgqk,bkhd->bqhgd', p_loc, v_blk)
                + jnp.einsum('bhgqm,bmhd->bqhgd', p_ctx, vc))

    out = lax.map(one_block, (jnp.arange(nb), qb))
    return out.transpose(1, 0, 2, 3, 4, 5).reshape(B, L, Hq, hd)


def _gla_chunk_scan(q, k, v, g, s0):
    B, L, H, dk = q.shape
    dv = v.shape[-1]
    C = GLA_CHUNK
    n = L // C

    def to_chunks(t):
        return t.astype(jnp.float32).reshape(B, n, C, H, t.shape[-1]).transpose(1, 0, 3, 2, 4)

    causal = jnp.tril(jnp.ones((C, C), dtype=bool))[None, None, :, :, None]

    def step(S, inp):
        qc, kc, vc, gc = inp
        b = jnp.cumsum(gc, axis=2)
        o_inter = jnp.einsum('bhik,bhkv->bhiv', qc * jnp.exp(b), S)
        diff = b[:, :, :, None, :] - b[:, :, None, :, :]
        decay = jnp.exp(jnp.where(causal, diff, -jnp.inf))
        a = jnp.einsum('bhik,bhjk,bhijk->bhij', qc, kc, decay)
        o = o_inter + jnp.einsum('bhij,bhjv->bhiv', a, vc)
        b_last = b[:, :, -1:, :]
        S_new = (jnp.exp(b_last[:, :, 0, :])[..., None] * S
                 + jnp.einsum('bhjk,bhjv->bhkv', kc * jnp.exp(b_last - b), vc))
        return S_new, o

    S_fin, o = lax.scan(step, s0, (to_chunks(q), to_chunks(k), to_chunks(v), to_chunks(g)))
    o = o.transpose(1, 0, 3, 2, 4).reshape(B, L, H, dv)
    return o, S_fin


def _gla_bidirectional(q, k, v, gf, gb, q_c, k_c, v_c, gf_c, gb_c):
    B, _, H, dk = q.shape
    dv = v.shape[-1]
    s0 = jnp.zeros((B, H, dk, dv), jnp.float32)
    fl = lambda t: jnp.flip(t, axis=1)
    oc_f, sc_f = _gla_chunk_scan(q_c, k_c, v_c, gf_c, s0)
    oc_b, sc_b = _gla_chunk_scan(fl(q_c), fl(k_c), fl(v_c), fl(gb_c), s0)
    ol_f, _ = _gla_chunk_scan(q, k, v, gf, sc_f)
    ol_b, _ = _gla_chunk_scan(fl(q), fl(k), fl(v), fl(gb), sc_b)
    return ol_f + fl(ol_b), oc_f + fl(oc_b)


def _gla_output(o, r, g):
    of = o.astype(jnp.float32)
    of = of * lax.rsqrt(jnp.mean(of * of, axis=-1, keepdims=True) + EPS) * g.astype(jnp.float32)
    of = of.reshape(r.shape)
    return (of * jax.nn.silu(r.astype(jnp.float32))).astype(r.dtype)


def _squared_relu_mlp(h, w1, w2):
    return jnp.square(jax.nn.relu(h @ w1)) @ w2


def _mixers(z, zc, rpb, wg_f, bg_f, wg_b, bg_b, gla_g, sink, rope, with_ctx):
    qa, ka, va, qg, kg, vg, rg, gfl, gbl, qs, ks, vs = _split_cols(z)
    qa_c, ka_c, va_c, qg_c, kg_c, vg_c, rg_c, gfl_c, gbl_c, qs_c, ks_c, vs_c = _split_cols(zc)
    B, L = z.shape[0], z.shape[1]
    M = zc.shape[1]
    ka_ch, va_ch = _heads(ka_c, NA_HEADS), _heads(va_c, NA_HEADS)
    ya = _neighbourhood_attention(_heads(qa, NA_HEADS), _heads(ka, NA_HEADS), _heads(va, NA_HEADS),
                                  ka_ch, va_ch, rpb)
    gate = lambda u, w, b: _heads(jax.nn.log_sigmoid((u @ w + b).astype(jnp.float32)) / GLA_GATE_NORM, GLA_HEADS)
    qscale = GLA_DK ** -0.5
    ob, ob_c = _gla_bidirectional(
        _heads(qg, GLA_HEADS) * qscale, _heads(kg, GLA_HEADS), _heads(vg, GLA_HEADS),
        gate(gfl, wg_f, bg_f), gate(gbl, wg_b, bg_b),
        _heads(qg_c, GLA_HEADS) * qscale, _heads(kg_c, GLA_HEADS), _heads(vg_c, GLA_HEADS),
        gate(gfl_c, wg_f, bg_f), gate(gbl_c, wg_b, bg_b))
    yb = _gla_output(ob, rg, gla_g)
    ks_ch, vs_ch = _heads(ks_c, SWA_KV_HEADS), _heads(vs_c, SWA_KV_HEADS)
    ys = _window_attention(_apply_rope(_heads(qs, SWA_Q_HEADS), rope), _apply_rope(_heads(ks, SWA_KV_HEADS), rope),
                           _heads(vs, SWA_KV_HEADS), ks_ch, vs_ch, sink)
    y = jnp.concatenate([ya.reshape(B, L, NA_W), yb, ys.reshape(B, L, SWA_Q_W)], axis=-1)
    if not with_ctx:
        return y, None
    ya_c = _dense_attention(_heads(qa_c, NA_HEADS), ka_ch, va_ch, None)
    yb_c = _gla_output(ob_c, rg_c, gla_g)
    ys_c = _dense_attention(_heads(qs_c, SWA_Q_HEADS), ks_ch, vs_ch, sink)
    yc = jnp.concatenate([ya_c.reshape(B, M, NA_W), yb_c, ys_c.reshape(B, M, SWA_Q_W)], axis=-1)
    return y, yc


def _layer(x, xc, c_act, cc_act, w_mod, b_mod, n1, n2, w_in, rpb, wg_f, bg_f, wg_b, bg_b, gla_g, sink,
           w_out, w_ff1, w_ff2, rope, with_ctx):
    mod = (c_act @ w_mod + b_mod)[:, None, :]
    modc = (cc_act @ w_mod + b_mod)[None, None, :]
    sh1, sc1, ga1, sh2, sc2, ga2 = jnp.split(mod, 6, axis=-1)
    csh1, csc1, cga1, csh2, csc2, cga2 = jnp.split(modc, 6, axis=-1)
    h = _rmsnorm(x, n1) * (1 + sc1) + sh1
    hc = _rmsnorm(xc, n1) * (1 + csc1) + csh1
    y, yc = _mixers(h @ w_in, hc @ w_in, rpb, wg_f, bg_f, wg_b, bg_b, gla_g, sink, rope, with_ctx)
    x = x + ga1 * (y @ w_out)
    h2 = _rmsnorm(x, n2) * (1 + sc2) + sh2
    x = x + ga2 * _squared_relu_mlp(h2, w_ff1, w_ff2)
    if with_ctx:
        xc = xc + cga1 * (yc @ w_out)
        hc2 = _rmsnorm(xc, n2) * (1 + csc2) + csh2
        xc = xc + cga2 * _squared_relu_mlp(hc2, w_ff1, w_ff2)
    return x, xc


def setup_inputs(seed: int = 0) -> dict:
    key = jax.random.key(seed)
    ks = jax.random.split(key, 20)
    f32 = jnp.float32
    nrm = lambda k, shape, s: jax.random.normal(k, shape, f32) * s
    return {
        "x": nrm(ks[0], (BATCH, SEQ, D_MODEL), 1.0),
        "c": nrm(ks[1], (BATCH, D_MODEL), 1.0),
        "ctx": nrm(ks[2], (BATCH, CTX_LEN, D_MODEL), 1.0),
        "c_ctx": nrm(ks[3], (D_MODEL,), 1.0),
        "w_mod": nrm(ks[4], (DEPTH, D_MODEL, 6 * D_MODEL), 0.5 * D_MODEL ** -0.5),
        "b_mod": nrm(ks[5], (DEPTH, 6 * D_MODEL), 0.02),
        "norm1_g": 1.0 + nrm(ks[6], (DEPTH, D_MODEL), 0.02),
        "norm2_g": 1.0 + nrm(ks[7], (DEPTH, D_MODEL), 0.02),
        "w_in": nrm(ks[8], (DEPTH, D_MODEL, D_IN), D_MODEL ** -0.5),
        "na_rpb": nrm(ks[9], (DEPTH, NA_HEADS, 2 * NA_KH - 1, 2 * NA_KW - 1), 0.1),
        "gla_wg_fwd": nrm(ks[10], (DEPTH, GLA_GATE_RANK, GLA_QK_W), GLA_GATE_RANK ** -0.5),
        "gla_bg_fwd": nrm(ks[11], (DEPTH, GLA_QK_W), 0.1),
        "gla_wg_bwd": nrm(ks[12], (DEPTH, GLA_GATE_RANK, GLA_QK_W), GLA_GATE_RANK ** -0.5),
        "gla_bg_bwd": nrm(ks[13], (DEPTH, GLA_QK_W), 0.1),
        "gla_norm_g": 1.0 + nrm(ks[14], (DEPTH, GLA_DV), 0.02),
        "swa_sink": nrm(ks[15], (DEPTH, SWA_Q_HEADS), 0.5),
        "w_out": nrm(ks[16], (DEPTH, D_MIX, D_MODEL), D_MIX ** -0.5),
        "w_ff1": nrm(ks[17], (DEPTH, D_MODEL, D_FF), D_MODEL ** -0.5),
        "w_ff2": nrm(ks[18], (DEPTH, D_FF, D_MODEL), D_FF ** -0.5),
        "final_norm_g": 1.0 + nrm(ks[19], (D_MODEL,), 0.02),
    }


def reference(x, c, ctx, c_ctx, w_mod, b_mod, norm1_g, norm2_g, w_in, na_rpb, gla_wg_fwd, gla_bg_fwd,
              gla_wg_bwd, gla_bg_bwd, gla_norm_g, swa_sink, w_out, w_ff1, w_ff2, final_norm_g):
    rope = _axial_rope(x.shape[1])
    c_act = jax.nn.silu(c)
    cc_act = jax.nn.silu(c_ctx)
    xc = ctx
    for i in range(DEPTH):
        x, xc = _layer(x, xc, c_act, cc_act, w_mod[i], b_mod[i], norm1_g[i], norm2_g[i], w_in[i], na_rpb[i],
                       gla_wg_fwd[i], gla_bg_fwd[i], gla_wg_bwd[i], gla_bg_bwd[i], gla_norm_g[i], swa_sink[i],
                       w_out[i], w_ff1[i], w_ff2[i], rope, i < DEPTH - 1)
    return _rmsnorm(x, final_norm_g)
```

```python
import numpy as np
from contextlib import ExitStack
import concourse.bass as bass
import concourse.mybir as mybir
from concourse.bass_utils import run_bass_kernel_spmd

F32 = mybir.dt.float32
BF16 = mybir.dt.bfloat16
AF = mybir.ActivationFunctionType
ALU = mybir.AluOpType
AX = mybir.AxisListType

NEG = -1.0e30
EPS = 1e-6


class Buf:
    __slots__ = ("name", "writers", "readers", "dsem", "dram")

    def __init__(self, name, dram=False):
        self.name = name
        self.dram = dram
        self.writers = {}
        self.readers = {}
        self.dsem = None


class K:
    ENGS = ("pe", "act", "dve", "pool", "sp")

    def __init__(self, nc, n_dma_sems=90):
        self.nc = nc
        self.obj = {"pe": nc.tensor, "act": nc.scalar, "dve": nc.vector, "pool": nc.gpsimd, "sp": nc.sync}
        self.sem, self.count, self.known, self._cms = {}, {}, {}, []
        for e in self.ENGS:
            cm = nc.semaphore("s_" + e)
            self.sem[e] = cm.__enter__()
            self._cms.append(cm)
            self.count[e] = 0
            self.known[e] = {}
        self.dpool = []
        for i in range(n_dma_sems):
            cm = nc.semaphore("d%d" % i)
            self.dpool.append([cm.__enter__(), 0, False])
            self._cms.append(cm)
        self.dbufs = []
        self.n_wait = 0
        self.n_ins = 0

    def close(self):
        for cm in reversed(self._cms):
            cm.__exit__(None, None, None)

    def _gather(self, reads, writes, pwrites, selfkey):
        deps = {}
        for r in reads:
            for k, v in r.writers.items():
                if deps.get(k, 0) < v:
                    deps[k] = v
        for w in writes:
            for d in (w.writers, w.readers):
                for k, v in d.items():
                    if deps.get(k, 0) < v:
                        deps[k] = v
        for w in pwrites:
            for k, v in w.readers.items():
                if deps.get(k, 0) < v:
                    deps[k] = v
            if not w.dram:
                for k, v in w.writers.items():
                    if k != selfkey and deps.get(k, 0) < v:
                        deps[k] = v
        return deps

    def _semof(self, key):
        return self.sem[key] if isinstance(key, str) else self.dpool[key][0]

    def _emit_waits(self, eng, deps):
        kn = self.known[eng]
        o = self.obj[eng]
        for k, v in deps.items():
            if k == eng and eng == "pe":
                continue
            if kn.get(k, 0) >= v:
                continue
            if not isinstance(k, str) and self.dpool[k][1] > v:
                v = self.dpool[k][1]
            o.wait_ge(self._semof(k), v)
            kn[k] = v
            self.n_wait += 1

    def _record(self, key, val, reads, writes, pwrites):
        for r in reads:
            if r.readers.get(key, 0) < val:
                r.readers[key] = val
        for w in writes:
            w.writers = {key: val}
            w.readers = {}
        for w in pwrites:
            if w.writers.get(key, 0) < val:
                w.writers[key] = val

    def op(self, eng, fn, reads=(), writes=(), pwrites=(), inc=True):
        self._emit_waits(eng, self._gather(reads, writes, pwrites, eng))
        ins = fn(self.obj[eng])
        self.n_ins += 1
        if inc:
            ins.then_inc(self.sem[eng], 1)
            self.count[eng] += 1
            val = self.count[eng]
        else:
            val = self.count[eng] + 1
        self._record(eng, val, reads, writes, pwrites)
        return ins

    def _kind_of(self, i):
        return "cc" if i < 4 else ("sw" if i < 28 else "hw")

    def _dsem_for(self, b, sw=False):
        if b.dsem is None:
            b.dsem = {}
        if sw not in b.dsem:
            for i, s in enumerate(self.dpool):
                if not s[2] and self._kind_of(i) == sw:
                    s[2] = True
                    b.dsem[sw] = i
                    if b not in self.dbufs:
                        self.dbufs.append(b)
                    break
            else:
                raise RuntimeError("out of DMA semaphores")
        return b.dsem[sw]

    def dma(self, q, out, in_, reads=(), writes=(), pwrites=(), sbuf=None, **kw):
        if q == "dve":
            q = "sp"
        di = self._dsem_for(sbuf, sw=("sw" if q == "pool" else "hw"))
        self._emit_waits(q, self._gather(reads, writes, pwrites, di))
        ins = self.obj[q].dma_start(out=out, in_=in_, **kw)
        s = self.dpool[di]
        s[1] += 16
        ins.then_inc(s[0], 16)
        self.n_ins += 1
        self._record(di, s[1], reads, writes, pwrites)
        return ins

    def custom(self, eng, fn, key_buf, reads=(), writes=(), pwrites=(), incv=1):
        di = self._dsem_for(key_buf, sw="cc")
        self._emit_waits(eng, self._gather(reads, writes, pwrites, di))
        s = self.dpool[di]
        ins = fn(self.obj[eng])
        s[1] += incv
        ins.then_inc(s[0], incv)
        self._record(di, s[1], reads, writes, pwrites)
        return ins

    def barrier(self, release=True):
        allk = {}
        for e in self.ENGS:
            if self.count[e] > 0:
                allk[e] = self.count[e]
        for i, s in enumerate(self.dpool):
            if s[1] > 0:
                allk[i] = s[1]
        for e in self.ENGS:
            self._emit_waits(e, dict(allk))
        if release:
            for b in self.dbufs:
                for di in b.dsem.values():
                    self.dpool[di][2] = False
                b.dsem = None
                b.writers = {}
                b.readers = {}
            self.dbufs = []


class Cfg:
    def __init__(self, D=2048, SEQ=16384, DFF=8192, L=2, debug=False, stop_after=None):
        self.D, self.SEQ, self.DFF, self.L = D, SEQ, DFF, L
        self.B = 2
        self.NCORE = 8
        self.CPB = 4
        self.T = SEQ // self.CPB
        self.M = 256
        self.KD = D // 128
        self.FC = DFF // 128
        slots = D // 128
        self.NAH = slots // 4
        self.GH = slots // 4
        self.SQ = slots // 2
        self.SKV = self.SQ // 4
        self.HP = self.GH // 2
        self.R = self.T // 64
        self.NP = self.R // 2
        self.NB = self.T // 128
        self.NCH = self.T // 64
        self.S = 6 * D // 8
        self.DMIX = D
        NAH, GH, SQ, SKV = self.NAH, self.GH, self.SQ, self.SKV
        self.c_qa = 0
        self.c_ka = self.c_qa + NAH
        self.c_qs = self.c_ka + NAH
        self.c_ks = self.c_qs + SQ
        self.c_qg = self.c_ks + SKV
        self.c_kg = self.c_qg + self.HP
        self.c_rg = self.c_kg + self.HP
        self.c_gate = self.c_rg + GH
        self.NFM = self.c_gate
        self.fm_cols = self.NFM * 128 + 32
        self.o_va = self.fm_cols
        self.o_vs = self.o_va + NAH * 128
        self.o_vg = self.o_vs + SKV * 128
        self.DIN = self.o_vg + GH * 128
        self.debug = debug
        self.stop_after = stop_after
        self.TEXT = (self.R + 8) * 64
        self.h_ka = 0
        self.h_va = self.h_ka + NAH * 256
        self.h_ks = self.h_va + 2 * NAH * 128
        self.h_vs = self.h_ks + SKV * 128
        self.HB = self.h_vs + SKV * 128
        self.PB = 2 * self.HB
        self.PF = self.HP * 2 * 129

    def perm_cols(self):
        NAH, GH, SQ, SKV = self.NAH, self.GH, self.SQ, self.SKV
        naw, gqk, gv, sqw, skvw = NAH * 128, GH * 64, GH * 128, SQ * 128, SKV * 128
        splits = [naw, naw, naw, gqk, gqk, gv, gv, 16, 16, sqw, skvw, skvw]
        names = ["qa", "ka", "va", "qg", "kg", "vg", "rg", "gfl", "gbl", "qs", "ks", "vs"]
        off = {}
        o = 0
        for n, s in zip(names, splits):
            off[n] = (o, s)
            o += s
        order = ["qa", "ka", "qs", "ks", "qg", "kg", "rg", "gfl", "gbl", "va", "vs", "vg"]
        idx = np.concatenate([np.arange(off[n][0], off[n][0] + off[n][1]) for n in order])
        assert idx.size == self.DIN == o
        return idx


def _na_bias_table(rpb_h, qrow0, krow0, nrows, ROWS):
    KH, KW, W = 8, 16, 64
    tab = np.full((128, nrows * 64), NEG, np.float32)
    qr = qrow0 + np.arange(128) // 64
    qc = np.arange(128) % 64
    rs = np.clip(qr - KH // 2, 0, ROWS - KH)
    cs = np.clip(qc - KW // 2, 0, W - KW)
    kr = krow0 + np.arange(nrows * 64) // 64
    kc = np.arange(nrows * 64) % 64
    ok = ((kr[None, :] >= rs[:, None]) & (kr[None, :] < rs[:, None] + KH) & (kr[None, :] >= 0) & (kr[None, :] < ROWS)
          & (kc[None, :] >= cs[:, None]) & (kc[None, :] < cs[:, None] + KW))
    ri = np.clip(kr[None, :] - qr[:, None] + KH - 1, 0, 2 * KH - 2)
    ci = np.clip(kc[None, :] - qc[:, None] + KW - 1, 0, 2 * KW - 2)
    vals = rpb_h[ri, ci]
    tab[ok] = vals[ok]
    return tab


def prep_inputs(cfg, inp):
    c = cfg
    D, T, KD, L = c.D, c.T, c.KD, c.L
    f32 = np.float32
    x = np.asarray(inp["x"], f32)
    ctx = np.asarray(inp["ctx"], f32)
    cvec = np.asarray(inp["c"], f32)
    cctx = np.asarray(inp["c_ctx"], f32)
    w_mod = np.asarray(inp["w_mod"], f32)
    b_mod = np.asarray(inp["b_mod"], f32)
    perm = c.perm_cols()
    w_in_p = np.ascontiguousarray(np.asarray(inp["w_in"], f32)[:, :, perm])
    w_out = np.ascontiguousarray(np.asarray(inp["w_out"], f32))
    w_ff1 = np.ascontiguousarray(np.asarray(inp["w_ff1"], f32))
    w_ff2 = np.ascontiguousarray(np.asarray(inp["w_ff2"], f32))
    rpb = np.asarray(inp["na_rpb"], f32)
    ROWS = c.SEQ // 64

    def colT(v):
        v = np.asarray(v, f32)
        lead = v.shape[:-1]
        return np.ascontiguousarray(np.moveaxis(v.reshape(lead + (KD, 128)), -1, 0))

    C3 = np.stack([cvec[0], cvec[1], cctx, np.zeros_like(cctx)], 0)
    cT = np.ascontiguousarray(np.transpose(C3.reshape(4, KD, 128), (2, 1, 0)))
    n1T = colT(inp["norm1_g"])
    n2T = colT(inp["norm2_g"])
    fgT = colT(inp["final_norm_g"])
    sinkT = np.ascontiguousarray(np.broadcast_to(np.asarray(inp["swa_sink"], f32)[None], (128, L, c.SQ)))
    wgf = np.asarray(inp["gla_wg_fwd"], f32)
    wgb = np.asarray(inp["gla_wg_bwd"], f32)
    wgf_pad = np.zeros((L, 32, c.GH * 64), f32); wgf_pad[:, 0:16] = wgf
    wgb_pad = np.zeros((L, 32, c.GH * 64), f32); wgb_pad[:, 16:32] = wgb
    bgfT = np.ascontiguousarray(np.transpose(np.asarray(inp["gla_bg_fwd"], f32).reshape(L, c.HP, 128), (2, 0, 1)))
    bgbT = np.ascontiguousarray(np.transpose(np.asarray(inp["gla_bg_bwd"], f32).reshape(L, c.HP, 128), (2, 0, 1)))
    ggT = np.ascontiguousarray(np.asarray(inp["gla_norm_g"], f32).T)
    ident = np.eye(128, dtype=f32)
    pm = np.zeros((128, 128), f32)
    for m in range(128):
        blk = m // 32
        partner = m + 32 if blk % 2 == 0 else m - 32
        pm[partner, m] = 1.0
    keep = np.ones((128, 512), f32); keep[:, 0::64] = 0.0
    jj = np.arange(64)[:, None]; ii = np.arange(64)[None, :]
    cf = (ii >= jj).astype(f32); cb = (jj >= ii).astype(f32)
    cm4 = np.zeros((128, 2, 2, 128), f32)
    for d_, cmx in enumerate((cf, cb)):
        for half in range(2):
            cm4[half * 64:(half + 1) * 64, d_, half, :] = np.tile(cmx, (1, 2))
    nab_reg = np.stack([np.stack([_na_bias_table(rpb[l, h], 8, 4, 10, ROWS) for h in range(c.NAH)], 0) for l in range(L)], 0)
    nab_reg = np.ascontiguousarray(np.transpose(nab_reg, (0, 2, 1, 3)))
    n_freq = 32
    inv = (10000.0 ** (-np.arange(n_freq, dtype=np.float32) / n_freq)).astype(f32)
    in_maps = []
    for core in range(c.NCORE):
        b, q = core // c.CPB, core % c.CPB
        row0 = q * c.R
        m = {}
        m["x"] = np.ascontiguousarray(x[b, q * T:(q + 1) * T])
        m["ctxb"] = np.ascontiguousarray(ctx[b])
        m["cT"] = cT
        m["wmod"] = np.ascontiguousarray(w_mod[:, :, core * c.S:(core + 1) * c.S])
        m["bmod"] = np.ascontiguousarray(b_mod[:, None, core * c.S:(core + 1) * c.S])
        sel = np.zeros((128, 2), f32); sel[:, b] = 1.0
        m["sel"] = sel
        m["n1T"], m["n2T"], m["fgT"] = n1T, n2T, fgT
        m["w_in"], m["w_out"], m["w_ff1"], m["w_ff2"] = w_in_p, w_out, w_ff1, w_ff2
        m["nab_reg"] = nab_reg
        sp = np.zeros((L, 128, c.NAH, 4, 768), f32)
        for l in range(L):
            for h in range(c.NAH):
                for si, p in enumerate([0, 1, c.NP - 2, c.NP - 1]):
                    krow0 = row0 - 4 if si < 2 else row0 + c.R - 8
                    sp[l, :, h, si] = _na_bias_table(rpb[l, h], row0 + 2 * p, krow0, 12, ROWS)
        m["nab_sp"] = sp
        rel = np.arange(384)[None, :] - 128 - np.arange(128)[:, None]
        inwin = np.abs(rel) <= 128
        swm = np.zeros((128, 3, 384), f32)
        for si, n in enumerate([0, 1, c.NB - 1]):
            kpos = (q * T + n * 128) - 128 + np.arange(384)
            valid = inwin & ((kpos >= 0) & (kpos < c.SEQ))[None, :]
            swm[:, si] = np.where(valid, 0.0, NEG)
        m["swm"] = swm
        m["sinkT"] = sinkT
        m["wgf"], m["wgb"], m["bgfT"], m["bgbT"], m["ggT"] = wgf_pad, wgb_pad, bgfT, bgbT, ggT
        m["ident"], m["permm"], m["keep"], m["cm4"] = ident, pm, keep, cm4
        t = q * T + np.arange(T)
        rowf = (t // 64).astype(f32); colf = (t % 64).astype(f32)
        ar = rowf[None, :] * inv[:, None]; ac = colf[None, :] * inv[:, None]
        cosr, sinr, cosc, sinc = np.cos(ar), np.sin(ar), np.cos(ac), np.sin(ac)
        m["ropeC"] = np.concatenate([cosr, cosr, cosc, cosc], 0).astype(f32)
        m["ropeS"] = np.concatenate([-sinr, sinr, -sinc, sinc], 0).astype(f32)
        wprev = np.zeros((128, 8), f32); wnext = np.zeros((128, 8), f32)
        if q > 0: wprev[:, core - 1] = 1.0
        if q < c.CPB - 1: wnext[:, core + 1] = 1.0
        selb = np.zeros((128, c.CPB, 8), f32)
        for j in range(c.CPB):
            selb[:, j, b * c.CPB + j] = 1.0
        pos = np.zeros((128, c.CPB), f32); pos[:, q] = 1.0
        m["wprev"], m["wnext"], m["selb"], m["pos"] = wprev, wnext, selb, pos
        in_maps.append(m)
    return in_maps


def build(cfg):
    c = cfg
    D, T, KD, L, M, FC = c.D, c.T, c.KD, c.L, c.M, c.FC
    NAH, GH, SQ, SKV, HP = c.NAH, c.GH, c.SQ, c.SKV, c.HP
    nc = bass.Bass("TRN2", target_bir_lowering=False)
    k = K(nc)
    dbg = c.debug

    def din(name, shape, dt=F32):
        return nc.dram_tensor(name, list(shape), dt, kind="ExternalInput").ap()

    def dscr(name, shape, dt):
        kind = "ExternalOutput" if dbg else "Internal"
        return nc.dram_tensor(name, list(shape), dt, kind=kind).ap()

    def dint(name, shape, dt):
        return nc.dram_tensor(name, list(shape), dt, kind="Internal").ap()

    x_in = din("x", [T, D]); ctx_in = din("ctxb", [M, D]); cT_in = din("cT", [128, KD, 4])
    wmod_in = din("wmod", [L, D, c.S]); bmod_in = din("bmod", [L, 1, c.S]); sel_in = din("sel", [128, 2])
    n1T_in = din("n1T", [128, L, KD]); n2T_in = din("n2T", [128, L, KD]); fgT_in = din("fgT", [128, KD])
    w_in = din("w_in", [L, D, c.DIN]); w_out = din("w_out", [L, D, D])
    w_ff1 = din("w_ff1", [L, D, c.DFF]); w_ff2 = din("w_ff2", [L, c.DFF, D])
    nab_reg_in = din("nab_reg", [L, 128, NAH, 640]); nab_sp_in = din("nab_sp", [L, 128, NAH, 4, 768])
    swm_in = din("swm", [128, 3, 384]); sinkT_in = din("sinkT", [128, L, SQ])
    wgf_in = din("wgf", [L, 32, GH * 64]); wgb_in = din("wgb", [L, 32, GH * 64])
    bgfT_in = din("bgfT", [128, L, HP]); bgbT_in = din("bgbT", [128, L, HP]); ggT_in = din("ggT", [128, L])
    ident_in = din("ident", [128, 128]); perm_in = din("permm", [128, 128]); keep_in = din("keep", [128, 512])
    cm4_in = din("cm4", [128, 2, 2, 128])
    ropeC_in = din("ropeC", [128, T]); ropeS_in = din("ropeS", [128, T])
    wprev_in = din("wprev", [128, 8]); wnext_in = din("wnext", [128, 8])
    selb_in = din("selb", [128, c.CPB, 8]); pos_in = din("pos", [128, c.CPB])
    out_d = nc.dram_tensor("out", [T, D], F32, kind="ExternalOutput").ap()

    xT = [dscr("xT%d" % i, [128, KD, T], F32) for i in range(L + 1)]
    x1T = [dscr("x1T%d" % i, [128, KD, T], F32) for i in range(L)]
    h2T = [dscr("h2T%d" % i, [128, KD, T], BF16) for i in range(L)]
    yT = [dscr("yT%d" % i, [128, KD, T], BF16) for i in range(L)]
    qaT = [dscr("qaT%d" % i, [128, NAH, T], BF16) for i in range(L)]
    kaT = [dscr("kaT%d" % i, [128, NAH, T], BF16) for i in range(L)]
    qsT = [dscr("qsT%d" % i, [128, SQ, T], BF16) for i in range(L)]
    ksT = [dscr("ksT%d" % i, [128, SKV, T], BF16) for i in range(L)]
    qgT = [dscr("qgT%d" % i, [128, HP, T], F32) for i in range(L)]
    kgT = [dscr("kgT%d" % i, [128, HP, T], F32) for i in range(L)]
    rgT = [dscr("rgT%d" % i, [128, GH, T], BF16) for i in range(L)]
    gtT = [dscr("gtT%d" % i, [32, T], F32) for i in range(L)]
    va = [dscr("va%d" % i, [T, NAH * 128], BF16) for i in range(L)]
    vs = [dscr("vs%d" % i, [T, SKV * 128], BF16) for i in range(L)]
    vg = [dscr("vg%d" % i, [T, GH * 128], BF16) for i in range(L)]
    obT = [dscr("obT%d" % i, [128, GH, T], F32) for i in range(L)]
    qcT = [dscr("qcT%d" % i, [128, 2, HP, T], BF16) for i in range(L)]
    totd = [dscr("tot%d" % i, [128, 2, HP, c.NCH], F32) for i in range(L)]
    cqaT = [dscr("cqaT%d" % i, [128, NAH, M], BF16) for i in range(L)]
    ckaT = [dscr("ckaT%d" % i, [128, NAH, M], BF16) for i in range(L)]
    cqsT = [dscr("cqsT%d" % i, [128, SQ, M], BF16) for i in range(L)]
    cksT = [dscr("cksT%d" % i, [128, SKV, M], BF16) for i in range(L)]
    cqgT = [dscr("cqgT%d" % i, [128, HP, M], F32) for i in range(L)]
    ckgT = [dscr("ckgT%d" % i, [128, HP, M], F32) for i in range(L)]
    crgT = [dscr("crgT%d" % i, [128, GH, M], BF16) for i in range(L)]
    cgtT = [dscr("cgtT%d" % i, [32, M], F32) for i in range(L)]
    cva = [dscr("cva%d" % i, [M, NAH * 128], BF16) for i in range(L)]
    cvs = [dscr("cvs%d" % i, [M, SKV * 128], BF16) for i in range(L)]
    cvg = [dscr("cvg%d" % i, [M, GH * 128], BF16) for i in range(L)]
    cobT = [dscr("cobT%d" % i, [128, GH, M], F32) for i in range(L)]
    cyT = [dscr("cyT%d" % i, [128, KD, M], BF16) for i in range(L)]
    xcT = [dscr("xcT%d" % i, [128, KD, M], F32) for i in range(L + 1)]
    cx1T = [dscr("cx1T%d" % i, [128, KD, M], F32) for i in range(L)]
    ch2T = [dscr("ch2T%d" % i, [128, KD, M], BF16) for i in range(L)]
    cstate = [dscr("cstate%d" % i, [128, 2, HP, 128], F32) for i in range(L)]
    mod_src = dint("mod_src", [3 * L, c.S], F32)
    mod_all = dint("mod_all", [8 * 3 * L, c.S], F32)
    pub_bf = [dint("pub_bf%d" % i, [128, c.PB], BF16) for i in range(L)]
    pub_f = [dint("pub_f%d" % i, [128, c.PF], F32) for i in range(L)]
    gat_bf = [dint("gat_bf%d" % i, [8 * 128, c.PB], BF16) for i in range(L)]
    gat_f = [dint("gat_f%d" % i, [8 * 128, c.PF], F32) for i in range(L)]

    DB = {}

    def db(ap):
        n = ap.name
        if n not in DB:
            DB[n] = Buf(n, dram=True)
        return DB[n]

    es_glob = ExitStack()

    uid = [0]

    def sbt(es, name, shape, dt):
        uid[0] += 1
        return es.enter_context(nc.sbuf_tensor("sb%d_%s" % (uid[0], name), list(shape), dt))

    def pst(es, name, shape, dt):
        uid[0] += 1
        return es.enter_context(nc.psum_tensor("ps%d_%s" % (uid[0], name), list(shape), dt))

    ident_f = sbt(es_glob, "ident_f", [128, 128], F32); b_ident_f = Buf("ident_f")
    ident_b = sbt(es_glob, "ident_b", [128, 128], BF16); b_ident_b = Buf("ident_b")
    meanD = sbt(es_glob, "meanD", [128, 128], BF16); b_meanD = Buf("meanD")
    mean128 = sbt(es_glob, "mean128", [128, 128], BF16); b_mean128 = Buf("mean128")
    MODC = sbt(es_glob, "MODC", [128, L, 2, 6, KD], F32); b_MODC = Buf("MODC")
    fgT = sbt(es_glob, "fgT", [128, KD], F32); b_fgT = Buf("fgT")
    k.dma("sp", ident_f[:], ident_in, writes=[b_ident_f], sbuf=b_ident_f)
    k.dma("pool", ident_b[:], ident_in, writes=[b_ident_b], sbuf=b_ident_b)
    k.dma("sp", fgT[:], fgT_in, writes=[b_fgT], sbuf=b_fgT)
    k.op("dve", lambda e: e.memset(meanD[:], 1.0 / D), writes=[b_meanD])
    k.op("dve", lambda e: e.memset(mean128[:], 1.0 / 128), writes=[b_mean128])

    def norm_tile(es_name, xt, b_xt, n, Acol, Bcol, b_cols, hout, b_hout, scr):
        sq, b_sq, ps, b_ps, rstd, b_rstd, tmp, b_tmp = scr
        k.op("act", lambda e: e.activation(out=sq[:, :, 0:n], in_=xt[:, :, 0:n], func=AF.Square), reads=[b_xt], writes=[b_sq])
        for kc in range(KD):
            k.op("pe", lambda e, kc=kc: e.matmul(ps[:, 0:n], lhsT=meanD[:], rhs=sq[:, kc, 0:n], start=(kc == 0), stop=(kc == KD - 1)),
                 reads=[b_sq, b_meanD], writes=[b_ps] if kc == 0 else (), pwrites=() if kc == 0 else [b_ps], inc=(kc == KD - 1))
        k.op("act", lambda e: e.activation(out=rstd[:, 0:n], in_=ps[:, 0:n], func=AF.Sqrt, bias=EPSC[:, 0:1], scale=1.0), reads=[b_ps, b_EPSC], writes=[b_rstd])
        k.op("dve", lambda e: e.reciprocal(out=rstd[:, 0:n], in_=rstd[:, 0:n]), reads=[b_rstd], writes=[b_rstd])
        for kc in range(KD):
            j = kc % 2
            k.op("dve", lambda e, kc=kc, j=j: e.scalar_tensor_tensor(out=tmp[j][:, 0:n], in0=xt[:, kc, 0:n], scalar=Acol[:, kc:kc + 1], in1=rstd[:, 0:n], op0=ALU.mult, op1=ALU.mult),
                 reads=[b_xt, b_cols, b_rstd], writes=[b_tmp[j]])
            k.op("act", lambda e, kc=kc, j=j: e.activation(out=hout[:, kc, 0:n], in_=tmp[j][:, 0:n], func=AF.Identity, bias=Bcol[:, kc:kc + 1], scale=1.0),
                 reads=[b_tmp[j], b_cols], pwrites=[b_hout])

    ONEC = sbt(es_glob, "ONEC", [128, 1], F32)
    EPSC = sbt(es_glob, "EPSC", [128, 1], F32); b_EPSC = Buf("EPSC")
    k.op("dve", lambda e: e.memset(ONEC[:], 1.0), writes=[b_EPSC])
    k.op("dve", lambda e: e.memset(EPSC[:], EPS), writes=[b_EPSC])

    def norm_scratch(es, pfx, ps, b_ps):
        sq = sbt(es, pfx + "sq", [128, KD, 512], BF16)
        rstd = sbt(es, pfx + "rstd", [128, 512], F32)
        t0 = sbt(es, pfx + "t0", [128, 512], F32); t1 = sbt(es, pfx + "t1", [128, 512], F32)
        return (sq, Buf(pfx + "sq"), ps, b_ps, rstd, Buf(pfx + "rstd"), [t0, t1], [Buf(pfx + "t0"), Buf(pfx + "t1")])

    STOP = c.stop_after

    with ExitStack() as es:
        cTt = sbt(es, "cTt", [128, KD, 4], F32); b_cTt = Buf("cTt")
        cact = sbt(es, "cact", [128, KD, 4], F32); b_cact = Buf("cact")
        sig = sbt(es, "sig", [128, KD, 4], F32); b_sig = Buf("sig")
        ones1 = sbt(es, "ones1", [1, 4], F32); b_ones1 = Buf("ones1")
        NW = 512
        wts = [sbt(es, "wm%d" % i, [128, KD, NW], F32) for i in range(2)]; b_wts = [Buf("wm0"), Buf("wm1")]
        bmt = sbt(es, "bmt", [1, L * c.S], F32); b_bmt = Buf("bmt")
        mrow = sbt(es, "mrow", [3, L * c.S], F32); b_mrow = Buf("mrow")
        psm = [pst(es, "psm%d" % i, [128, 512], F32) for i in range(2)]; b_psm = [Buf("psm0"), Buf("psm1")]
        k.dma("sp", cTt[:], cT_in, writes=[b_cTt], sbuf=b_cTt)
        k.dma("sp", bmt[:].rearrange("o (l s) -> o l s", l=L), bmod_in.rearrange("l o s -> o l s"), writes=[b_bmt], sbuf=b_bmt)
        k.op("dve", lambda e: e.memset(ones1[:], 1.0), writes=[b_ones1])
        k.op("act", lambda e: e.activation(out=sig[:], in_=cTt[:], func=AF.Sigmoid), reads=[b_cTt], writes=[b_sig])
        k.op("dve", lambda e: e.tensor_tensor(out=cact[:], in0=cTt[:], in1=sig[:], op=ALU.mult), reads=[b_cTt, b_sig], writes=[b_cact])
        it = 0
        for l in range(L):
            for n0 in range(0, c.S, NW):
                nsz = min(NW, c.S - n0)
                j = it % 2; it += 1
                k.dma("sp", wts[j][:, :, 0:nsz], wmod_in[l].rearrange("(kc p) s -> p kc s", p=128)[:, :, n0:n0 + nsz], writes=[b_wts[j]], sbuf=b_wts[j])
                for kc in range(KD):
                    k.op("pe", lambda e, kc=kc, j=j, nsz=nsz: e.matmul(psm[j][0:3, 0:nsz], lhsT=cact[:, kc, 0:3], rhs=wts[j][:, kc, 0:nsz], start=(kc == 0), stop=False),
                         reads=[b_cact, b_wts[j]], writes=[b_psm[j]] if kc == 0 else (), pwrites=() if kc == 0 else [b_psm[j]], inc=False)
                k.op("pe", lambda e, j=j, nsz=nsz, l=l, n0=n0: e.matmul(psm[j][0:3, 0:nsz], lhsT=ones1[0:1, 0:3], rhs=bmt[0:1, l * c.S + n0:l * c.S + n0 + nsz], start=False, stop=True),
                     reads=[b_ones1, b_bmt], pwrites=[b_psm[j]])
                k.op("act", lambda e, j=j, nsz=nsz, l=l, n0=n0: e.activation(out=mrow[0:3, l * c.S + n0:l * c.S + n0 + nsz], in_=psm[j][0:3, 0:nsz], func=AF.Copy),
                     reads=[b_psm[j]], pwrites=[b_mrow])
        k.dma("sp", mod_src.rearrange("(l r) s -> r l s", r=3), mrow[0:3, :].rearrange("r (l s) -> r l s", l=L), reads=[b_mrow], writes=[db(mod_src)], sbuf=b_mrow)
        b_cc = Buf("cc_mod")
        k.custom("pool", lambda e: e.collective_compute("AllGather", ALU.bypass, replica_groups=[list(range(8))], ins=[mod_src.opt()], outs=[mod_all.opt()]),
                 b_cc, reads=[db(mod_src)], writes=[db(mod_all)])
        NQ = 3 * L
        mall = sbt(es, "mall", [NQ, 8 * c.S], F32); b_mall = Buf("mall")
        k.dma("sp", mall[:].rearrange("q (r s) -> q r s", r=8), mod_all.rearrange("(r q) s -> q r s", q=NQ), reads=[db(mod_all)], writes=[b_mall], sbuf=b_mall)
        NCK = 6 * D // 128
        modT = sbt(es, "modT", [128, NCK, 8], F32); b_modT = Buf("modT")
        pstt = pst(es, "pstt", [128, 64, 8], F32); b_pstt = Buf("pstt")
        for c0 in range(0, NCK, 64):
            cn = min(64, NCK - c0)
            for ci in range(cn):
                k.op("pe", lambda e, ci=ci, c0=c0: e.transpose(out=pstt[:, ci, 0:NQ], in_=mall[0:NQ, (c0 + ci) * 128:(c0 + ci + 1) * 128], identity=ident_f[0:NQ, 0:NQ]),
                     reads=[b_mall, b_ident_f], writes=[b_pstt] if ci == 0 else (), pwrites=() if ci == 0 else [b_pstt], inc=(ci == cn - 1))
            k.op("dve", lambda e, c0=c0, cn=cn: e.tensor_copy(out=modT[:, c0:c0 + cn, 0:NQ], in_=pstt[:, 0:cn, 0:NQ]), reads=[b_pstt], pwrites=[b_modT])
        selt = sbt(es, "selt", [128, 2], F32); b_selt = Buf("selt")
        n1t = sbt(es, "n1t", [128, L, KD], F32); n2t = sbt(es, "n2t", [128, L, KD], F32); b_nt = Buf("nt")
        mown = sbt(es, "mown", [128, NCK], F32); b_mown = Buf("mown")
        k.dma("sp", selt[:], sel_in, writes=[b_selt], sbuf=b_selt)
        k.dma("sp", n1t[:], n1T_in, pwrites=[b_nt], sbuf=b_nt)
        k.dma("sp", n2t[:], n2T_in, pwrites=[b_nt], sbuf=b_nt)
        for l in range(L):
            for kind in range(2):
                if kind == 0:
                    k.op("dve", lambda e, l=l: e.tensor_scalar(out=mown[:], in0=modT[:, :, l * 3 + 0], scalar1=selt[:, 0:1], scalar2=None, op0=ALU.mult), reads=[b_modT, b_selt], writes=[b_mown])
                    k.op("dve", lambda e, l=l: e.scalar_tensor_tensor(out=mown[:], in0=modT[:, :, l * 3 + 1], scalar=selt[:, 1:2], in1=mown[:], op0=ALU.mult, op1=ALU.add), reads=[b_modT, b_selt, b_mown], writes=[b_mown])
                else:
                    k.op("dve", lambda e, l=l: e.tensor_copy(out=mown[:], in_=modT[:, :, l * 3 + 2]), reads=[b_modT], writes=[b_mown])
                def mv(v):
                    return mown[:, v * KD:(v + 1) * KD]
                k.op("dve", lambda e, l=l, kind=kind: e.scalar_tensor_tensor(out=MODC[:, l, kind, 0, :], in0=mv(1), scalar=1.0, in1=n1t[:, l, :], op0=ALU.add, op1=ALU.mult), reads=[b_mown, b_nt], pwrites=[b_MODC])
                k.op("dve", lambda e, l=l, kind=kind: e.tensor_copy(out=MODC[:, l, kind, 1, :], in_=mv(0)), reads=[b_mown], pwrites=[b_MODC])
                k.op("dve", lambda e, l=l, kind=kind: e.tensor_copy(out=MODC[:, l, kind, 2, :], in_=mv(2)), reads=[b_mown], pwrites=[b_MODC])
                k.op("dve", lambda e, l=l, kind=kind: e.scalar_tensor_tensor(out=MODC[:, l, kind, 3, :], in0=mv(4), scalar=1.0, in1=n2t[:, l, :], op0=ALU.add, op1=ALU.mult), reads=[b_mown, b_nt], pwrites=[b_MODC])
                k.op("dve", lambda e, l=l, kind=kind: e.tensor_copy(out=MODC[:, l, kind, 4, :], in_=mv(3)), reads=[b_mown], pwrites=[b_MODC])
                k.op("dve", lambda e, l=l, kind=kind: e.tensor_copy(out=MODC[:, l, kind, 5, :], in_=mv(5)), reads=[b_mown], pwrites=[b_MODC])
        if dbg:
            modc_d = nc.dram_tensor("modc_dbg", [128, L * 2 * 6 * KD], F32, kind="ExternalOutput").ap()
            k.dma("sp", modc_d, MODC[:].rearrange("p l a v k -> p (l a v k)"), reads=[b_MODC], writes=[db(modc_d)], sbuf=b_MODC)
        k.barrier()

    def transpose_in(src, ntok, dst):
        with ExitStack() as es:
            xin = [sbt(es, "xin%d" % i, [128, D], F32) for i in range(2)]; b_xin = [Buf("xin0"), Buf("xin1")]
            xst = [sbt(es, "xst%d" % i, [128, KD, 128], F32) for i in range(2)]; b_xst = [Buf("xst0"), Buf("xst1")]
            pt = [pst(es, "ptx%d" % i, [128, 512], F32) for i in range(4)]; b_pt = [Buf("ptx%d" % i) for i in range(4)]
            g = 0
            for i in range(ntok // 128):
                j = i % 2
                k.dma("sp", xin[j][:], src[i * 128:(i + 1) * 128, :], writes=[b_xin[j]], sbuf=b_xin[j])
                for k4 in range(0, KD, 4):
                    pj = g % 4; g += 1
                    for q in range(4):
                        k.op("pe", lambda e, q=q, k4=k4, pj=pj, j=j: e.transpose(out=pt[pj][:, q * 128:(q + 1) * 128], in_=xin[j][:, (k4 + q) * 128:(k4 + q + 1) * 128], identity=ident_f[:]),
                             reads=[b_xin[j], b_ident_f], writes=[b_pt[pj]] if q == 0 else (), pwrites=() if q == 0 else [b_pt[pj]], inc=(q == 3))
                    eng = "act" if (k4 // 4) % 2 == 0 else "dve"
                    if eng == "act":
                        k.op("act", lambda e, k4=k4, pj=pj, j=j: e.activation(out=xst[j][:, k4:k4 + 4, :], in_=pt[pj][:].rearrange("p (a b) -> p a b", a=4), func=AF.Copy), reads=[b_pt[pj]], pwrites=[b_xst[j]])
                    else:
                        k.op("dve", lambda e, k4=k4, pj=pj, j=j: e.tensor_copy(out=xst[j][:, k4:k4 + 4, :], in_=pt[pj][:].rearrange("p (a b) -> p a b", a=4)), reads=[b_pt[pj]], pwrites=[b_xst[j]])
                k.dma("sp", dst[:, :, i * 128:(i + 1) * 128], xst[j][:], reads=[b_xst[j]], pwrites=[db(dst)], sbuf=b_xst[j])
            k.barrier()

    transpose_in(x_in, T, xT[0])
    transpose_in(ctx_in, M, xcT[0])


    perm_f = sbt(es_glob, "perm_f", [128, 128], F32); b_perm_f = Buf("perm_f")
    k.dma("sp", perm_f[:], perm_in, writes=[b_perm_f], sbuf=b_perm_f)

    def proj_phase(l, src, NT, kind, dst, rope):
        ST = min(NT, 2048)
        NS = 256
        hd_s = 128.0 ** -0.5
        dk_s = 64.0 ** -0.5
        with ExitStack() as es:
            hT = sbt(es, "hT", [128, KD, ST], BF16); b_hT = Buf("hT")
            xt = [sbt(es, "xt%d" % i, [128, KD, NS], F32) for i in range(2)]; b_xt = [Buf("xt0"), Buf("xt1")]
            wt = [sbt(es, "wt%d" % i, [128, KD, 512], BF16) for i in range(2)]; b_wt = [Buf("wt0"), Buf("wt1")]
            ps_n = pst(es, "psn", [128, 512], F32); b_ps_n = Buf("psn")
            scr = norm_scratch(es, "nb", ps_n, b_ps_n)
            ps_f = [pst(es, "psf%d" % i, [128, 512], F32) for i in range(3)]; b_ps_f = [Buf("psf%d" % i) for i in range(3)]
            ps_r = [pst(es, "psr%d" % i, [128, 512], F32) for i in range(2)]; b_ps_r = [Buf("psr%d" % i) for i in range(2)]
            ps_t = [pst(es, "pst%d" % i, [128, 512], F32) for i in range(2)]; b_ps_t = [Buf("pst%d" % i) for i in range(2)]
            stg = [sbt(es, "stg%d" % i, [128, 512], F32) for i in range(4)]; b_stg = [Buf("stg%d" % i) for i in range(4)]
            if rope:
                rc = sbt(es, "rc", [128, ST], F32); rs = sbt(es, "rs", [128, ST], F32); b_rt = Buf("ropetab")
                qf = [sbt(es, "qf%d" % i, [128, 512], F32) for i in range(2)]; b_qf = [Buf("qf0"), Buf("qf1")]
                t1 = sbt(es, "t1", [128, 512], F32); b_t1 = Buf("t1")
                t2 = sbt(es, "t2", [128, 512], F32); b_t2 = Buf("t2")
            Acol = MODC[:, l, kind, 0, :]; Bcol = MODC[:, l, kind, 1, :]
            cnt = dict(f=0, r=0, t=0, s=0, w=0, x=0)
            wsrc = w_in[l].rearrange("(kc p) c -> p kc c", p=128)
            groups = []
            c0 = 0
            while c0 < c.fm_cols:
                csz = min(512, c.fm_cols - c0)
                if c.fm_cols - (c0 + csz) == 32:
                    csz += 0
                groups.append(("fm", c0, csz)); c0 += csz
            for (o, w) in ((c.o_va, NAH * 128), (c.o_vs, SKV * 128), (c.o_vg, GH * 128)):
                g0 = 0
                while g0 < w:
                    gs = min(512, w - g0)
                    groups.append(("tm", o + g0, gs)); g0 += gs

            def stage_out(eng, si, dt, n, dest_ap):
                pass

            for s0 in range(0, NT, ST):
                if rope:
                    k.dma("sp", rc[:, 0:ST], ropeC_in[:, s0:s0 + ST], writes=[b_rt], sbuf=b_rt)
                    k.dma("sp", rs[:, 0:ST], ropeS_in[:, s0:s0 + ST], pwrites=[b_rt], sbuf=b_rt)
                for n0 in range(0, ST, NS):
                    j = cnt["x"] % 2; cnt["x"] += 1
                    k.dma("sp", xt[j][:], src[:, :, s0 + n0:s0 + n0 + NS], writes=[b_xt[j]], sbuf=b_xt[j])
                    norm_tile("nb", xt[j], b_xt[j], NS, Acol, Bcol, b_MODC, hT[:, :, n0:n0 + NS], b_hT, scr)
                for (gk, g0, gsz) in groups:
                    wj = cnt["w"] % 2; cnt["w"] += 1
                    k.dma("pool", wt[wj][:, :, 0:gsz], wsrc[:, :, g0:g0 + gsz], writes=[b_wt[wj]], sbuf=b_wt[wj])
                    if gk == "fm":
                        for cc0 in range(0, gsz, 128):
                            cw = min(128, gsz - cc0)
                            cid = (g0 + cc0) // 128
                            for n0 in range(0, ST, 512):
                                n = min(512, ST - n0)
                                pj = cnt["f"] % 3; cnt["f"] += 1
                                ps = ps_f[pj]; bps = b_ps_f[pj]
                                for kc in range(KD):
                                    k.op("pe", lambda e, kc=kc, ps=ps, wj=wj, cc0=cc0, cw=cw, n0=n0, n=n: e.matmul(ps[0:cw, 0:n], lhsT=wt[wj][:, kc, cc0:cc0 + cw], rhs=hT[:, kc, n0:n0 + n], start=(kc == 0), stop=(kc == KD - 1)),
                                         reads=[b_wt[wj], b_hT], writes=[bps] if kc == 0 else (), pwrites=() if kc == 0 else [bps], inc=(kc == KD - 1))
                                tg = s0 + n0
                                sj = cnt["s"] % 4; cnt["s"] += 1
                                st_f = stg[sj]; st_b = stg[sj][:].bitcast(BF16); bst = b_stg[sj]
                                if cid < c.c_ka:
                                    h = cid - c.c_qa
                                    k.op("act", lambda e, ps=ps, n=n, st_b=st_b: e.activation(out=st_b[:, 0:n], in_=ps[:, 0:n], func=AF.Copy, scale=hd_s), reads=[bps], writes=[bst])
                                    k.dma("act", dst["qa"][:, h, tg:tg + n], st_b[:, 0:n], reads=[bst], pwrites=[db(dst["qa"])], sbuf=bst)
                                elif cid < c.c_qs:
                                    h = cid - c.c_ka
                                    k.op("dve", lambda e, ps=ps, n=n, st_b=st_b: e.tensor_copy(out=st_b[:, 0:n], in_=ps[:, 0:n]), reads=[bps], writes=[bst])
                                    k.dma("dve", dst["ka"][:, h, tg:tg + n], st_b[:, 0:n], reads=[bst], pwrites=[db(dst["ka"])], sbuf=bst)
                                elif cid < c.c_qg:
                                    isq = cid < c.c_ks
                                    h = cid - (c.c_qs if isq else c.c_ks)
                                    dd = dst["qs"] if isq else dst["ks"]
                                    sc = hd_s if isq else 1.0
                                    if not rope:
                                        k.op("act", lambda e, ps=ps, n=n, st_b=st_b, sc=sc: e.activation(out=st_b[:, 0:n], in_=ps[:, 0:n], func=AF.Copy, scale=sc), reads=[bps], writes=[bst])
                                        k.dma("act", dd[:, h, tg:tg + n], st_b[:, 0:n], reads=[bst], pwrites=[db(dd)], sbuf=bst)
                                    else:
                                        rj = cnt["r"] % 2; cnt["r"] += 1
                                        k.op("act", lambda e, ps=ps, n=n, rj=rj, sc=sc: e.activation(out=qf[rj][:, 0:n], in_=ps[:, 0:n], func=AF.Copy, scale=sc), reads=[bps], writes=[b_qf[rj]])
                                        k.op("pe", lambda e, rj=rj, n=n: e.matmul(ps_r[rj][:, 0:n], lhsT=perm_f[:], rhs=qf[rj][:, 0:n], start=True, stop=True), reads=[b_perm_f, b_qf[rj]], writes=[b_ps_r[rj]])
                                        k.op("dve", lambda e, rj=rj, n=n, n0=n0: e.tensor_tensor(out=t1[:, 0:n], in0=qf[rj][:, 0:n], in1=rc[:, n0:n0 + n], op=ALU.mult), reads=[b_qf[rj], b_rt], writes=[b_t1])
                                        k.op("dve", lambda e, rj=rj, n=n, n0=n0: e.tensor_tensor(out=t2[:, 0:n], in0=ps_r[rj][:, 0:n], in1=rs[:, n0:n0 + n], op=ALU.mult), reads=[b_ps_r[rj], b_rt], writes=[b_t2])
                                        k.op("dve", lambda e, n=n, st_b=st_b: e.tensor_tensor(out=st_b[:, 0:n], in0=t1[:, 0:n], in1=t2[:, 0:n], op=ALU.add), reads=[b_t1, b_t2], writes=[bst])
                                        k.dma("dve", dd[:, h, tg:tg + n], st_b[:, 0:n], reads=[bst], pwrites=[db(dd)], sbuf=bst)
                                elif cid < c.c_kg:
                                    h = cid - c.c_qg
                                    k.op("act", lambda e, ps=ps, n=n, st_f=st_f: e.activation(out=st_f[:, 0:n], in_=ps[:, 0:n], func=AF.Copy, scale=dk_s), reads=[bps], writes=[bst])
                                    k.dma("act", dst["qg"][:, h, tg:tg + n], st_f[:, 0:n], reads=[bst], pwrites=[db(dst["qg"])], sbuf=bst)
                                elif cid < c.c_rg:
                                    h = cid - c.c_kg
                                    k.op("dve", lambda e, ps=ps, n=n, st_f=st_f: e.tensor_copy(out=st_f[:, 0:n], in_=ps[:, 0:n]), reads=[bps], writes=[bst])
                                    k.dma("dve", dst["kg"][:, h, tg:tg + n], st_f[:, 0:n], reads=[bst], pwrites=[db(dst["kg"])], sbuf=bst)
                                elif cid < c.c_gate:
                                    h = cid - c.c_rg
                                    k.op("act", lambda e, ps=ps, n=n, st_b=st_b: e.activation(out=st_b[:, 0:n], in_=ps[:, 0:n], func=AF.Silu), reads=[bps], writes=[bst])
                                    k.dma("act", dst["rg"][:, h, tg:tg + n], st_b[:, 0:n], reads=[bst], pwrites=[db(dst["rg"])], sbuf=bst)
                                else:
                                    k.op("dve", lambda e, ps=ps, n=n, st_f=st_f: e.tensor_copy(out=st_f[0:32, 0:n], in_=ps[0:32, 0:n]), reads=[bps], writes=[bst])
                                    k.dma("dve", dst["gt"][0:32, tg:tg + n], st_f[0:32, 0:n], reads=[bst], pwrites=[db(dst["gt"])], sbuf=bst)
                    else:
                        if g0 >= c.o_vg:
                            dd = dst["vg"]; cb = g0 - c.o_vg
                        elif g0 >= c.o_vs:
                            dd = dst["vs"]; cb = g0 - c.o_vs
                        else:
                            dd = dst["va"]; cb = g0 - c.o_va
                        for n0 in range(0, ST, 128):
                            pj = cnt["t"] % 2; cnt["t"] += 1
                            ps = ps_t[pj]; bps = b_ps_t[pj]
                            for kc in range(KD):
                                k.op("pe", lambda e, kc=kc, ps=ps, wj=wj, n0=n0, gsz=gsz: e.matmul(ps[:, 0:gsz], lhsT=hT[:, kc, n0:n0 + 128], rhs=wt[wj][:, kc, 0:gsz], start=(kc == 0), stop=(kc == KD - 1)),
                                     reads=[b_wt[wj], b_hT], writes=[bps] if kc == 0 else (), pwrites=() if kc == 0 else [bps], inc=(kc == KD - 1))
                            sj = cnt["s"] % 4; cnt["s"] += 1
                            st_b = stg[sj][:].bitcast(BF16); bst = b_stg[sj]
                            eng = "act" if (cnt["t"] % 2 == 0) else "dve"
                            if eng == "act":
                                k.op("act", lambda e, ps=ps, gsz=gsz, st_b=st_b: e.activation(out=st_b[:, 0:gsz], in_=ps[:, 0:gsz], func=AF.Copy), reads=[bps], writes=[bst])
                            else:
                                k.op("dve", lambda e, ps=ps, gsz=gsz, st_b=st_b: e.tensor_copy(out=st_b[:, 0:gsz], in_=ps[:, 0:gsz]), reads=[bps], writes=[bst])
                            tg = s0 + n0
                            k.dma(eng, dd[tg:tg + 128, cb:cb + gsz], st_b[:, 0:gsz], reads=[bst], pwrites=[db(dd)], sbuf=bst)
            k.barrier()


    HALO = {}
    b_halo = Buf("halo")
    wprev = sbt(es_glob, "wprev", [128, 8], F32); wnext = sbt(es_glob, "wnext", [128, 8], F32); b_wpn = Buf("wpn")
    k.dma("sp", wprev[:], wprev_in, pwrites=[b_wpn], sbuf=b_wpn)
    k.dma("sp", wnext[:], wnext_in, pwrites=[b_wpn], sbuf=b_wpn)

    def publish_halos(l):
        pb = pub_bf[l]
        bk = Buf("pubdma")
        HB = c.HB
        for blk, (t_ka, t_va, t_ks, t_vs) in enumerate((((0, 256), (0, 256), (0, 128), (0, 128)), ((T - 256, T), (T - 256, T), (T - 128, T), (T - 128, T)))):
            o = blk * HB
            k.dma("sp", pb[:, o + c.h_ka:o + c.h_va].rearrange("p (h t) -> p h t", h=NAH), kaT[l][:, :, t_ka[0]:t_ka[1]], reads=[db(kaT[l])], pwrites=[db(pb)], sbuf=bk)
            k.dma("sp", pb[:, o + c.h_va:o + c.h_ks].rearrange("p (a c) -> p a c", a=2), va[l][t_va[0]:t_va[1], :].rearrange("(a p) c -> p a c", p=128), reads=[db(va[l])], pwrites=[db(pb)], sbuf=bk)
            k.dma("sp", pb[:, o + c.h_ks:o + c.h_vs].rearrange("p (h t) -> p h t", h=SKV), ksT[l][:, :, t_ks[0]:t_ks[1]], reads=[db(ksT[l])], pwrites=[db(pb)], sbuf=bk)
            k.dma("sp", pb[:, o + c.h_vs:o + HB], vs[l][t_vs[0]:t_vs[1], :], reads=[db(vs[l])], pwrites=[db(pb)], sbuf=bk)

    def exchange(l):
        b_cc = Buf("cc_x")
        k.custom("pool", lambda e: e.collective_compute("AllGather", ALU.bypass, replica_groups=[list(range(8))], ins=[pub_bf[l].opt()], outs=[gat_bf[l].opt()]),
                 b_cc, reads=[db(pub_bf[l])], writes=[db(gat_bf[l])])
        k.custom("pool", lambda e: e.collective_compute("AllGather", ALU.bypass, replica_groups=[list(range(8))], ins=[pub_f[l].opt()], outs=[gat_f[l].opt()]),
                 b_cc, reads=[db(pub_f[l])], writes=[db(gat_f[l])])
        with ExitStack() as es:
            G = sbt(es, "G", [128, 8, c.PB], BF16); b_G = Buf("G")
            k.dma("sp", G[:], gat_bf[l].rearrange("(r p) f -> p r f", p=128), reads=[db(gat_bf[l])], writes=[b_G], sbuf=b_G)
            HB = c.HB
            for (dstt, wsel, o) in ((HALO['P'], wprev, HB), (HALO['N'], wnext, 0)):
                for r in range(8):
                    if r == 0:
                        k.op("dve", lambda e, dstt=dstt, wsel=wsel, o=o, r=r: e.tensor_scalar(out=dstt[:], in0=G[:, r, o:o + HB], scalar1=wsel[:, r:r + 1], scalar2=None, op0=ALU.mult), reads=[b_G, b_wpn], writes=[b_halo])
                    else:
                        k.op("dve", lambda e, dstt=dstt, wsel=wsel, o=o, r=r: e.scalar_tensor_tensor(out=dstt[:], in0=G[:, r, o:o + HB], scalar=wsel[:, r:r + 1], in1=dstt[:], op0=ALU.mult, op1=ALU.add), reads=[b_G, b_wpn, b_halo], writes=[b_halo])
            k.barrier()

    def attn_unit(A, qT, rbufs, segs, vchunks, sink, out_ap, b_out):
        i = A["i"]; A["i"] += 1
        S = A["S"][i % 2]; bS = A["bS"][i % 2]
        Pt = A["P"][i % 2]; bP = A["bP"][i % 2]
        pT = A["pT"][i % 2]; bpT = A["bpT"][i % 2]
        PT = A["PT"][i % 2]; bPT = A["bPT"][i % 2]
        po = A["po"][:, (i % 4) * 128:(i % 4 + 1) * 128]; bpo = A["bpo"][i % 4]
        sm = A["sm"][i % 2]; bsm = A["bsm"][i % 2]
        col = 0
        first = True
        pieces = []
        for (kT, nk, bias) in segs:
            o = 0
            while o < nk:
                w = min(nk - o, 512 - (col % 512))
                pieces.append((kT, bias, o, w, col))
                o += w; col += w
        ntot = col
        npc = len(pieces)
        for pi, (kT, bias, o, w, cl) in enumerate(pieces):
            k.op("pe", lambda e, kT=kT, o=o, w=w, cl=cl, bias=bias: e.matmul(S[:, cl:cl + w], lhsT=qT, rhs=kT[:, o:o + w], start=True, stop=(bias is None)),
                 reads=rbufs, writes=[bS] if pi == 0 else (), pwrites=() if pi == 0 else [bS], inc=(pi == npc - 1 and bias is None))
            if bias is not None:
                k.op("pe", lambda e, o=o, w=w, cl=cl, bias=bias: e.matmul(S[:, cl:cl + w], lhsT=ident_b[:], rhs=bias[:, o:o + w], start=False, stop=True),
                     reads=rbufs + [b_ident_b], pwrites=[bS], inc=(pi == npc - 1))
        k.op("dve", lambda e: e.reduce_max(out=sm[:, 0:1], in_=S[:, 0:ntot], axis=AX.X), reads=[bS], writes=[bsm])
        if sink is not None:
            k.op("dve", lambda e: e.tensor_tensor(out=sm[:, 0:1], in0=sm[:, 0:1], in1=sink, op=ALU.max), reads=[bsm] + rbufs, writes=[bsm])
        k.op("dve", lambda e: e.tensor_scalar(out=sm[:, 1:2], in0=sm[:, 0:1], scalar1=-1.0, scalar2=None, op0=ALU.mult), reads=[bsm], writes=[bsm])
        k.op("act", lambda e: e.activation(out=Pt[:, 0:ntot], in_=S[:, 0:ntot], func=AF.Exp, bias=sm[:, 1:2], scale=1.0, accum_out=sm[:, 2:3]), reads=[bS, bsm], writes=[bP, bsm])
        if sink is not None:
            k.op("act", lambda e: e.activation(out=sm[:, 3:4], in_=sm[:, 1:2], func=AF.Exp, bias=sink, scale=1.0), reads=[bsm] + rbufs, writes=[bsm])
            k.op("dve", lambda e: e.tensor_tensor(out=sm[:, 2:3], in0=sm[:, 2:3], in1=sm[:, 3:4], op=ALU.add), reads=[bsm], writes=[bsm])
        k.op("dve", lambda e: e.reciprocal(out=sm[:, 4:5], in_=sm[:, 2:3]), reads=[bsm], writes=[bsm])
        k.op("dve", lambda e: e.tensor_scalar(out=Pt[:, 0:ntot], in0=Pt[:, 0:ntot], scalar1=sm[:, 4:5], scalar2=None, op0=ALU.mult), reads=[bP, bsm], writes=[bP])
        nv = len(vchunks)
        o = 0
        for vi, (v_ap, ki) in enumerate(vchunks):
            k.op("pe", lambda e, vi=vi, ki=ki, o=o: e.transpose(out=pT[0:ki, vi * 128:(vi + 1) * 128], in_=Pt[:, o:o + ki], identity=ident_b[:]),
                 reads=[bP, b_ident_b], writes=[bpT] if vi == 0 else (), pwrites=() if vi == 0 else [bpT], inc=(vi == nv - 1))
            o += ki
        assert o == ntot, (o, ntot)
        if i % 2 == 0:
            k.op("act", lambda e: e.activation(out=PT[:, 0:nv * 128], in_=pT[:, 0:nv * 128], func=AF.Copy), reads=[bpT], writes=[bPT])
        else:
            k.op("dve", lambda e: e.tensor_copy(out=PT[:, 0:nv * 128], in_=pT[:, 0:nv * 128]), reads=[bpT], writes=[bPT])
        for vi, (v_ap, ki) in enumerate(vchunks):
            k.op("pe", lambda e, vi=vi, ki=ki, v_ap=v_ap: e.matmul(po, lhsT=v_ap, rhs=PT[0:ki, vi * 128:(vi + 1) * 128], start=(vi == 0), stop=(vi == nv - 1)),
                 reads=[bPT] + rbufs, writes=[bpo] if vi == 0 else (), pwrites=() if vi == 0 else [bpo], inc=(vi == nv - 1))
        k.op("act", lambda e: e.activation(out=out_ap, in_=po, func=AF.Copy), reads=[bpo], pwrites=[b_out])

    def attn_resources(es):
        A = dict(i=0)
        A["S"] = [pst(es, "S%d" % i, [128, 1024], F32) for i in range(2)]; A["bS"] = [Buf("S0"), Buf("S1")]
        A["pT"] = [pst(es, "pT%d" % i, [128, 1024], BF16) for i in range(2)]; A["bpT"] = [Buf("pT0"), Buf("pT1")]
        A["po"] = pst(es, "po", [128, 512], F32); A["bpo"] = [Buf("po%d" % i) for i in range(4)]
        A["P"] = [sbt(es, "P%d" % i, [128, 1024], BF16) for i in range(2)]; A["bP"] = [Buf("P0"), Buf("P1")]
        A["PT"] = [sbt(es, "PT%d" % i, [128, 1024], BF16) for i in range(2)]; A["bPT"] = [Buf("PT0"), Buf("PT1")]
        A["sm"] = [sbt(es, "sm%d" % i, [128, 8], F32) for i in range(2)]; A["bsm"] = [Buf("sm0"), Buf("sm1")]
        return A

    def na_phase(l, with_ctx):
        TEXT = c.TEXT
        NTX = TEXT // 128
        with ExitStack() as es:
            A = attn_resources(es)
            kx = sbt(es, "kx", [128, NAH, TEXT], BF16); b_kx = Buf("kx")
            vx = sbt(es, "vx", [128, NTX, NAH * 128], BF16); b_vx = Buf("vx")
            qa = sbt(es, "qa", [128, NAH, T], BF16); b_qa = Buf("qa")
            ck = sbt(es, "ck", [128, NAH, M], BF16); cv = sbt(es, "cv", [128, 2, NAH * 128], BF16); b_c = Buf("ckv")
            nbr = sbt(es, "nbr", [128, NAH, 640], BF16); nbs = sbt(es, "nbs", [128, NAH, 4, 768], BF16); b_nb = Buf("nb")
            yst = [sbt(es, "yst%d" % i, [128, NAH, 512], BF16) for i in range(2)]; b_yst = [Buf("yst0"), Buf("yst1")]
            k.dma("sp", kx[:, :, 256:256 + T], kaT[l], reads=[db(kaT[l])], pwrites=[b_kx], sbuf=b_kx)
            k.dma("sp", vx[:, 2:2 + T // 128, :], va[l].rearrange("(a p) c -> p a c", p=128), reads=[db(va[l])], pwrites=[b_vx], sbuf=b_vx)
            k.dma("sp", qa[:], qaT[l], reads=[db(qaT[l])], writes=[b_qa], sbuf=b_qa)
            k.dma("sp", ck[:], ckaT[l], reads=[db(ckaT[l])], pwrites=[b_c], sbuf=b_c)
            k.dma("sp", cv[:], cva[l].rearrange("(a p) c -> p a c", p=128), reads=[db(cva[l])], pwrites=[b_c], sbuf=b_c)
            k.dma("pool", nbr[:], nab_reg_in[l], pwrites=[b_nb], sbuf=b_nb)
            k.dma("pool", nbs[:], nab_sp_in[l], pwrites=[b_nb], sbuf=b_nb)
            k.op("dve", lambda e: e.tensor_copy(out=kx[:, :, 0:256], in_=HALO['P'][:, c.h_ka:c.h_va].rearrange("p (h t) -> p h t", h=NAH)), reads=[b_halo], pwrites=[b_kx])
            k.op("dve", lambda e: e.tensor_copy(out=kx[:, :, 256 + T:512 + T], in_=HALO['N'][:, c.h_ka:c.h_va].rearrange("p (h t) -> p h t", h=NAH)), reads=[b_halo], pwrites=[b_kx])
            k.op("dve", lambda e: e.tensor_copy(out=vx[:, 0:2, :], in_=HALO['P'][:, c.h_va:c.h_ks].rearrange("p (a c) -> p a c", a=2)), reads=[b_halo], pwrites=[b_vx])
            k.op("dve", lambda e: e.tensor_copy(out=vx[:, NTX - 2:NTX, :], in_=HALO['N'][:, c.h_va:c.h_ks].rearrange("p (a c) -> p a c", a=2)), reads=[b_halo], pwrites=[b_vx])
            rb = [b_kx, b_vx, b_qa, b_c, b_nb]
            NPp = c.NP
            for p in range(NPp):
                yj = (p // 4) % 2
                for h in range(NAH):
                    if p < 2:
                        start, ln, bias = 0, 768, nbs[:, h, p, :]
                    elif p >= NPp - 2:
                        start, ln, bias = (NPp - 2) * 128, 768, nbs[:, h, 2 + (p - (NPp - 2)), :]
                    else:
                        start, ln, bias = 128 * p, 640, nbr[:, h, :]
                    segs = [(kx[:, h, start:start + ln], ln, bias), (ck[:, h, :], M, None)]
                    vch = []
                    o = 0
                    while o < ln:
                        ki = min(128, ln - o)
                        vch.append((vx[0:ki, (start + o) // 128, h * 128:(h + 1) * 128], ki)); o += ki
                    vch += [(cv[:, 0, h * 128:(h + 1) * 128], 128), (cv[:, 1, h * 128:(h + 1) * 128], 128)]
                    attn_unit(A, qa[:, h, p * 128:(p + 1) * 128], rb, segs, vch, None, yst[yj][:, h, (p % 4) * 128:(p % 4 + 1) * 128], b_yst[yj])
                if p % 4 == 3 or p == NPp - 1:
                    p0 = (p // 4) * 4
                    nn = (p - p0 + 1) * 128
                    k.dma("act", yT[l][:, 0:NAH, p0 * 128:p0 * 128 + nn], yst[yj][:, :, 0:nn], reads=[b_yst[yj]], pwrites=[db(yT[l])], sbuf=b_yst[yj])
            if with_ctx:
                cq = sbt(es, "cq", [128, NAH, M], BF16); b_cq = Buf("cq")
                k.dma("sp", cq[:], cqaT[l], reads=[db(cqaT[l])], writes=[b_cq], sbuf=b_cq)
                cy = sbt(es, "cy", [128, NAH, M], BF16); b_cy = Buf("cy")
                for qt in range(M // 128):
                    for h in range(NAH):
                        segs = [(ck[:, h, :], M, None)]
                        vch = [(cv[:, 0, h * 128:(h + 1) * 128], 128), (cv[:, 1, h * 128:(h + 1) * 128], 128)]
                        attn_unit(A, cq[:, h, qt * 128:(qt + 1) * 128], rb + [b_cq], segs, vch, None, cy[:, h, qt * 128:(qt + 1) * 128], b_cy)
                k.dma("act", cyT[l][:, 0:NAH, :], cy[:], reads=[b_cy], pwrites=[db(cyT[l])], sbuf=b_cy)
            k.barrier()

    def swa_phase(l, with_ctx):
        G4 = SQ // SKV
        with ExitStack() as es:
            A = attn_resources(es)
            NBX = c.NB + 2
            kx = sbt(es, "skx", [128, SKV, NBX * 128], BF16); b_kx = Buf("skx")
            vx = sbt(es, "svx", [128, NBX, SKV * 128], BF16); b_vx = Buf("svx")
            qs = sbt(es, "qs", [128, SQ, T], BF16); b_qs = Buf("qs")
            ck = sbt(es, "sck", [128, SKV, M], BF16); cv = sbt(es, "scv", [128, 2, SKV * 128], BF16); b_c = Buf("sckv")
            msk = sbt(es, "msk", [128, 3, 384], BF16); b_msk = Buf("msk")
            snk = sbt(es, "snk", [128, L, SQ], F32); b_snk = Buf("snk")
            yst = [sbt(es, "syst%d" % i, [128, SQ, 512], BF16) for i in range(2)]; b_yst = [Buf("syst0"), Buf("syst1")]
            k.dma("sp", kx[:, :, 128:128 + T], ksT[l], reads=[db(ksT[l])], pwrites=[b_kx], sbuf=b_kx)
            k.dma("sp", vx[:, 1:1 + c.NB, :], vs[l].rearrange("(a p) c -> p a c", p=128), reads=[db(vs[l])], pwrites=[b_vx], sbuf=b_vx)
            k.dma("sp", qs[:], qsT[l], reads=[db(qsT[l])], writes=[b_qs], sbuf=b_qs)
            k.dma("sp", ck[:], cksT[l], reads=[db(cksT[l])], pwrites=[b_c], sbuf=b_c)
            k.dma("sp", cv[:], cvs[l].rearrange("(a p) c -> p a c", p=128), reads=[db(cvs[l])], pwrites=[b_c], sbuf=b_c)
            k.dma("pool", msk[:], swm_in, writes=[b_msk], sbuf=b_msk)
            k.dma("sp", snk[:], sinkT_in, writes=[b_snk], sbuf=b_snk)
            k.op("dve", lambda e: e.tensor_copy(out=kx[:, :, 0:128], in_=HALO['P'][:, c.h_ks:c.h_vs].rearrange("p (h t) -> p h t", h=SKV)), reads=[b_halo], pwrites=[b_kx])
            k.op("dve", lambda e: e.tensor_copy(out=kx[:, :, 128 + T:256 + T], in_=HALO['N'][:, c.h_ks:c.h_vs].rearrange("p (h t) -> p h t", h=SKV)), reads=[b_halo], pwrites=[b_kx])
            k.op("dve", lambda e: e.tensor_copy(out=vx[:, 0, :], in_=HALO['P'][:, c.h_vs:c.HB]), reads=[b_halo], pwrites=[b_vx])
            k.op("dve", lambda e: e.tensor_copy(out=vx[:, NBX - 1, :], in_=HALO['N'][:, c.h_vs:c.HB]), reads=[b_halo], pwrites=[b_vx])
            rb = [b_kx, b_vx, b_qs, b_c, b_msk, b_snk]
            for n in range(c.NB):
                yj = (n // 4) % 2
                si = 0 if n == 0 else (2 if n == c.NB - 1 else 1)
                for hq in range(SQ):
                    kv = hq // G4
                    segs = [(kx[:, kv, n * 128:n * 128 + 384], 384, msk[:, si, :]), (ck[:, kv, :], M, None)]
                    vch = [(vx[:, n + j, kv * 128:(kv + 1) * 128], 128) for j in range(3)]
                    vch += [(cv[:, 0, kv * 128:(kv + 1) * 128], 128), (cv[:, 1, kv * 128:(kv + 1) * 128], 128)]
                    attn_unit(A, qs[:, hq, n * 128:(n + 1) * 128], rb, segs, vch, snk[:, l, hq:hq + 1], yst[yj][:, hq, (n % 4) * 128:(n % 4 + 1) * 128], b_yst[yj])
                if n % 4 == 3 or n == c.NB - 1:
                    n0 = (n // 4) * 4
                    nn = (n - n0 + 1) * 128
                    k.dma("act", yT[l][:, NAH + GH:NAH + GH + SQ, n0 * 128:n0 * 128 + nn], yst[yj][:, :, 0:nn], reads=[b_yst[yj]], pwrites=[db(yT[l])], sbuf=b_yst[yj])
            if with_ctx:
                cq = sbt(es, "scq", [128, SQ, M], BF16); b_cq = Buf("scq")
                k.dma("sp", cq[:], cqsT[l], reads=[db(cqsT[l])], writes=[b_cq], sbuf=b_cq)
                cy = sbt(es, "scy", [128, SQ, M], BF16); b_cy = Buf("scy")
                for qt in range(M // 128):
                    for hq in range(SQ):
                        kv = hq // G4
                        segs = [(ck[:, kv, :], M, None)]
                        vch = [(cv[:, 0, kv * 128:(kv + 1) * 128], 128), (cv[:, 1, kv * 128:(kv + 1) * 128], 128)]
                        attn_unit(A, cq[:, hq, qt * 128:(qt + 1) * 128], rb + [b_cq], segs, vch, snk[:, l, hq:hq + 1], cy[:, hq, qt * 128:(qt + 1) * 128], b_cy)
                k.dma("act", cyT[l][:, NAH + GH:NAH + GH + SQ, :], cy[:], reads=[b_cy], pwrites=[db(cyT[l])], sbuf=b_cy)
            k.barrier()


    keep_t = sbt(es_glob, "keep_t", [128, 512], F32); cmask = sbt(es_glob, "cmask", [128, 2, 2, 128], F32); b_gc = Buf("glaconst")
    k.dma("sp", keep_t[:], keep_in, pwrites=[b_gc], sbuf=b_gc)
    k.dma("sp", cmask[:], cm4_in, pwrites=[b_gc], sbuf=b_gc)
    ggc = sbt(es_glob, "ggc", [128, L], F32); b_ggc = Buf("ggc")
    k.dma("sp", ggc[:], ggT_in, writes=[b_ggc], sbuf=b_ggc)
    I16 = 1.0 / 16.0
    bgall = sbt(es_glob, "bgall", [128, 2, L, HP], F32); b_bgall = Buf("bgall")
    k.dma("sp", bgall[:, 0, :, :], bgfT_in, pwrites=[b_bgall], sbuf=b_bgall)
    k.dma("sp", bgall[:, 1, :, :], bgbT_in, pwrites=[b_bgall], sbuf=b_bgall)

    def gla_local(l, NT, S_, ob_dst, qc_dst, tot_dst, st_dst, pubf):
        NCHl = NT // 64
        NTI = NT // 128
        TP = min(512, NT)
        for hp in range(HP):
            with ExitStack() as es:
                qtl = [sbt(es, "qtl%d" % d, [128, NT], BF16) for d in range(2)]
                khat = [[sbt(es, "khat%d%d" % (d, h), [128, NT], BF16) for h in range(2)] for d in range(2)]
                ktm = [[sbt(es, "ktm%d%d" % (d, hf), [128, NTI, 128], BF16) for hf in range(2)] for d in range(2)]
                b_z = Buf("glazero")
                for d in range(2):
                    for h in range(2):
                        k.op("pool", lambda e, d=d, h=h: e.memset(khat[d][h][:], 0.0), pwrites=[b_z])
                        k.op("pool", lambda e, d=d, h=h: e.memset(ktm[d][h][:], 0.0), pwrites=[b_z])
                dec = [sbt(es, "dec%d" % d, [128, NCHl], F32) for d in range(2)]
                tot = [sbt(es, "tot%d" % d, [128, NCHl], F32) for d in range(2)]
                b_pre = [Buf("pre0"), Buf("pre1")]
                vgt = sbt(es, "vgt", [128, NTI, 256], BF16); b_vgt = Buf("vgt")
                oacc = sbt(es, "oacc", [128, 2, NT], F32); b_oacc = Buf("oacc")
                Sf = [sbt(es, "Sf%d" % d, [128, 128], F32) for d in range(2)]; b_Sf = [Buf("Sf0"), Buf("Sf1")]
                Sb = [[sbt(es, "Sb%d%d" % (d, h), [128, 128], BF16) for h in range(2)] for d in range(2)]; b_Sb = [Buf("Sb0"), Buf("Sb1")]
                Asb = [[sbt(es, "Asb%d%d" % (d, j), [128, 128], BF16) for j in range(2)] for d in range(2)]
                b_Asb = [[Buf("Asb%d%d" % (d, j)) for j in range(2)] for d in range(2)]
                wg = sbt(es, "wg", [128, 2, 128], F32); nbg = sbt(es, "nbg", [128, 2], F32); b_wg = Buf("wg")
                k.op("dve", lambda e: e.memset(wg[:], 0.0), writes=[b_wg])
                k.dma("sp", wg[0:32, 0, :], wgf_in[l][:, hp * 128:(hp + 1) * 128], pwrites=[b_wg], sbuf=b_wg)
                k.dma("sp", wg[0:32, 1, :], wgb_in[l][:, hp * 128:(hp + 1) * 128], pwrites=[b_wg], sbuf=b_wg)
                k.op("dve", lambda e: e.tensor_scalar(out=nbg[:, 0:1], in0=bgall[:, 0, l, hp:hp + 1], scalar1=-1.0, scalar2=None, op0=ALU.mult), reads=[b_bgall], pwrites=[b_wg])
                k.op("dve", lambda e: e.tensor_scalar(out=nbg[:, 1:2], in0=bgall[:, 1, l, hp:hp + 1], scalar1=-1.0, scalar2=None, op0=ALU.mult), reads=[b_bgall], pwrites=[b_wg])
                k.dma("sp", vgt[:], S_["vg"][:, hp * 256:(hp + 1) * 256].rearrange("(a p) c -> p a c", p=128), reads=[db(S_["vg"])], writes=[b_vgt], sbuf=b_vgt)
                with ExitStack() as es2:
                    psu = pst(es2, "psu", [128, 512], F32); b_psu = Buf("psu")
                    ptk = pst(es2, "ptk", [128, 512], BF16); b_ptk = Buf("ptk")
                    qgt = [sbt(es2, "qgt%d" % i, [128, 512], F32) for i in range(2)]
                    kgt = [sbt(es2, "kgt%d" % i, [128, 512], F32) for i in range(2)]
                    gtt = [sbt(es2, "gtt%d" % i, [128, 512], F32) for i in range(2)]
                    b_in = [Buf("gin0"), Buf("gin1")]
                    b_gz = Buf("gttz")
                    for i in range(2):
                        k.op("dve", lambda e, i=i: e.memset(gtt[i][:], 0.0), pwrites=[b_gz])
                    ee = sbt(es2, "ee", [128, 512], F32); b_ee = Buf("ee")
                    aa = sbt(es2, "aa", [128, 512], F32); b_aa = Buf("aa")
                    bc = sbt(es2, "bc", [128, 512], F32); b_bc = Buf("bc")
                    dlt = sbt(es2, "dlt", [128, 512], F32); b_dlt = Buf("dlt")
                    rr = sbt(es2, "rr", [128, 512], F32); b_rr = Buf("rr")
                    E = [sbt(es2, "E%d" % i, [128, 512], F32) for i in range(2)]; b_E = [Buf("E0"), Buf("E1")]
                    ktl = sbt(es2, "ktl", [128, 512], BF16); b_ktl = Buf("ktl")
                    ei = [0]

                    def expmul(src, b_src, scale, other, b_other, out_ap, b_out, n, pw=True, kh=None):
                        j = ei[0] % 2; ei[0] += 1
                        k.op("act", lambda e: e.activation(out=E[j][:, 0:n], in_=src, func=AF.Exp, scale=scale), reads=[b_src], writes=[b_E[j]])
                        if kh is not None:
                            d_, t0_ = kh
                            for h_ in range(2):
                                k.op("dve", lambda e, h_=h_: e.tensor_tensor(out=khat[d_][h_][h_ * 64:(h_ + 1) * 64, t0_:t0_ + n], in0=other[h_ * 64:(h_ + 1) * 64, :], in1=E[j][h_ * 64:(h_ + 1) * 64, 0:n], op=ALU.mult), reads=[b_E[j], b_other, b_z], pwrites=[b_out])
                            return
                        k.op("dve", lambda e: e.tensor_tensor(out=out_ap, in0=other, in1=E[j][:, 0:n], op=ALU.mult), reads=[b_E[j], b_other], pwrites=[b_out] if pw else (), writes=() if pw else [b_out])

                    for ti_, t0 in enumerate(range(0, NT, TP)):
                        n = min(TP, NT - t0)
                        ncn = n // 64
                        c0 = t0 // 64
                        j = ti_ % 2
                        k.dma("sp", qgt[j][:, 0:n], S_["qg"][:, hp, t0:t0 + n], reads=[db(S_["qg"])], writes=[b_in[j]], sbuf=b_in[j])
                        k.dma("sp", kgt[j][:, 0:n], S_["kg"][:, hp, t0:t0 + n], reads=[db(S_["kg"])], pwrites=[b_in[j]], sbuf=b_in[j])
                        k.dma("sp", gtt[j][0:32, 0:n], S_["gt"][:, t0:t0 + n], reads=[db(S_["gt"]), b_gz], pwrites=[b_in[j]], sbuf=b_in[j])
                        for d in range(2):
                            k.op("pe", lambda e, d=d: e.matmul(psu[:, 0:n], lhsT=wg[:, d, :], rhs=gtt[j][:, 0:n], start=True, stop=True), reads=[b_wg, b_in[j]], writes=[b_psu])
                            k.op("act", lambda e, d=d: e.activation(out=ee[:, 0:n], in_=psu[:, 0:n], func=AF.Exp, bias=nbg[:, d:d + 1], scale=-1.0), reads=[b_psu, b_wg], writes=[b_ee])
                            k.op("act", lambda e: e.activation(out=aa[:, 0:n], in_=ee[:, 0:n], func=AF.Ln, bias=ONEC[:, 0:1], scale=1.0), reads=[b_ee, b_EPSC], writes=[b_aa])
                            k.op("dve", lambda e: e.tensor_tensor_scan(out=bc[:, 0:n], data0=keep_t[:, 0:n], data1=aa[:, 0:n], initial=0.0, op0=ALU.mult, op1=ALU.add), reads=[b_aa, b_gc], writes=[b_bc])
                            k.op("dve", lambda e, d=d: e.tensor_copy(out=tot[d][:, c0:c0 + ncn], in_=bc[:, 63:n:64]), reads=[b_bc], pwrites=[b_pre[d]])
                            k.op("act", lambda e, d=d: e.activation(out=dec[d][:, c0:c0 + ncn], in_=bc[:, 63:n:64], func=AF.Exp, scale=-I16), reads=[b_bc], pwrites=[b_pre[d]])
                            totb = tot[d][:, c0:c0 + ncn].unsqueeze(2).to_broadcast([128, ncn, 64])
                            k.op("dve", lambda e, totb=totb: e.tensor_tensor(out=dlt[:, 0:n].rearrange("p (c t) -> p c t", t=64), in0=totb, in1=bc[:, 0:n].rearrange("p (c t) -> p c t", t=64), op=ALU.subtract), reads=[b_bc, b_pre[d]], writes=[b_dlt])
                            if d == 0:
                                expmul(bc[:, 0:n], b_bc, -I16, qgt[j][:, 0:n], b_in[j], qtl[0][:, t0:t0 + n], b_pre[0], n)
                                expmul(bc[:, 0:n], b_bc, I16, kgt[j][:, 0:n], b_in[j], None, b_pre[0], n, kh=(0, t0))
                                expmul(dlt[:, 0:n], b_dlt, -I16, kgt[j][:, 0:n], b_in[j], ktl[:, 0:n], b_ktl, n, pw=False)
                            else:
                                k.op("dve", lambda e: e.tensor_tensor(out=rr[:, 0:n], in0=dlt[:, 0:n], in1=aa[:, 0:n], op=ALU.add), reads=[b_dlt, b_aa], writes=[b_rr])
                                expmul(rr[:, 0:n], b_rr, -I16, qgt[j][:, 0:n], b_in[j], qtl[1][:, t0:t0 + n], b_pre[1], n)
                                expmul(rr[:, 0:n], b_rr, I16, kgt[j][:, 0:n], b_in[j], None, b_pre[1], n, kh=(1, t0))
                                k.op("dve", lambda e: e.tensor_tensor(out=rr[:, 0:n], in0=bc[:, 0:n], in1=aa[:, 0:n], op=ALU.subtract), reads=[b_bc, b_aa, b_E[0], b_E[1]], writes=[b_rr])
                                expmul(rr[:, 0:n], b_rr, -I16, kgt[j][:, 0:n], b_in[j], ktl[:, 0:n], b_ktl, n, pw=False)
                            nt4 = n // 128
                            for q4 in range(nt4):
                                k.op("pe", lambda e, q4=q4: e.transpose(out=ptk[:, q4 * 128:(q4 + 1) * 128], in_=ktl[:, q4 * 128:(q4 + 1) * 128], identity=ident_b[:]),
                                     reads=[b_ktl, b_ident_b], writes=[b_ptk] if q4 == 0 else (), pwrites=() if q4 == 0 else [b_ptk], inc=(q4 == nt4 - 1))
                            for hf in range(2):
                                k.op("act", lambda e, d=d, hf=hf: e.activation(out=ktm[d][hf][hf * 64:(hf + 1) * 64, t0 // 128:t0 // 128 + nt4, :], in_=ptk[hf * 64:(hf + 1) * 64, 0:nt4 * 128].rearrange("p (a b) -> p a b", b=128), func=AF.Copy), reads=[b_ptk, b_z], pwrites=[b_pre[d]])
                    for d in range(2):
                        if qc_dst is not None:
                            k.dma("sp", qc_dst[:, d, hp, :], qtl[d][:], reads=[b_pre[d]], pwrites=[db(qc_dst)], sbuf=b_pre[d])
                            k.dma("sp", tot_dst[:, d, hp, :], tot[d][:], reads=[b_pre[d]], pwrites=[db(tot_dst)], sbuf=b_pre[d])
                    k.barrier(release=False)
                with ExitStack() as es3:
                    psA = pst(es3, "psA", [128, 4, 128], F32); b_psA = [[Buf("psA%d%d" % (d, j)) for j in range(2)] for d in range(2)]
                    psS = [pst(es3, "psS%d" % i, [128, 2, 256], F32) for i in range(2)]; b_psS = [[Buf("psS%d%d" % (d, j)) for j in range(2)] for d in range(2)]
                    pso = [[pst(es3, "pso%d%d" % (d, j), [128, 512], F32) for j in range(2)] for d in range(2)]
                    b_pso = [[Buf("pso%d%d" % (d, j)) for j in range(2)] for d in range(2)]
                    for d in range(2):
                        k.op("dve", lambda e, d=d: e.memset(Sf[d][:], 0.0), writes=[b_Sf[d]])
                        for h in range(2):
                            k.op("dve", lambda e, d=d, h=h: e.memset(Sb[d][h][:], 0.0), writes=[b_Sb[d]] if h == 0 else (), pwrites=() if h == 0 else [b_Sb[d]])
                    written = [False] * ((NCHl + 3) // 4)
                    for s_ in range(NCHl):
                        for d in range(2):
                            cch = s_ if d == 0 else NCHl - 1 - s_
                            t0 = cch * 64; ti = cch // 2; hb = (cch % 2) * 64
                            par = s_ % 2
                            g = cch // 4; gp = g % 2; cc = cch % 4
                            A_ps = psA[:, d * 2 + par, :]
                            bA = b_psA[d][par]
                            half = cch % 2
                            for h in range(2):
                                k.op("pe", lambda e, h=h, A_ps=A_ps, d=d, t0=t0, ti=ti: e.matmul(A_ps[:, h * 64:(h + 1) * 64], lhsT=khat[d][h][:, ti * 128:(ti + 1) * 128], rhs=qtl[d][:, t0:t0 + 64], start=True, stop=True),
                                     reads=[b_pre[d]], writes=[bA] if h == 0 else (), pwrites=() if h == 0 else [bA], inc=(h == 1))
                            As = Asb[d][par]; bAs = b_Asb[d][par]
                            k.op("dve", lambda e, A_ps=A_ps, As=As, d=d, half=half: e.tensor_tensor(out=As[:], in0=A_ps, in1=cmask[:, d, half, :], op=ALU.mult), reads=[bA, b_gc], writes=[bAs])
                            po_t = pso[d][gp]; bpo = b_pso[d][gp]
                            firstg = (cc == 0) if d == 0 else (cc == 3 or cch == NCHl - 1)
                            for h in range(2):
                                col = (cc * 2 + h) * 64
                                k.op("pe", lambda e, h=h, col=col, po_t=po_t, d=d, t0=t0: e.matmul(po_t[:, col:col + 64], lhsT=Sb[d][h][:], rhs=qtl[d][:, t0:t0 + 64], start=True, stop=False),
                                     reads=[b_Sb[d], b_pre[d]], writes=[bpo] if (firstg and h == 0) else (), pwrites=() if (firstg and h == 0) else [bpo], inc=False)
                                k.op("pe", lambda e, h=h, col=col, po_t=po_t, As=As, hb=hb, ti=ti: e.matmul(po_t[:, col:col + 64], lhsT=vgt[:, ti, h * 128:(h + 1) * 128], rhs=As[:, h * 64:(h + 1) * 64], start=False, stop=True),
                                     reads=[bAs, b_vgt], pwrites=[bpo], inc=(h == 1))
                            pS = psS[d][:, par, :]; bpS = b_psS[d][par]
                            k.op("pe", lambda e, pS=pS, d=d, half=half, ti=ti: e.matmul(pS, lhsT=ktm[d][half][:, ti, :], rhs=vgt[:, ti, :], start=True, stop=True), reads=[b_pre[d], b_vgt], writes=[bpS])
                            for h in range(2):
                                k.op("dve", lambda e, h=h, pS=pS, d=d, cch=cch: e.scalar_tensor_tensor(out=Sf[d][h * 64:(h + 1) * 64, :], in0=Sf[d][h * 64:(h + 1) * 64, :], scalar=dec[d][h * 64:(h + 1) * 64, cch:cch + 1], in1=pS[h * 64:(h + 1) * 64, h * 128:(h + 1) * 128], op0=ALU.mult, op1=ALU.add),
                                     reads=[bpS, b_pre[d], b_Sf[d]] if h == 0 else [bpS, b_pre[d]], writes=[b_Sf[d]] if h == 0 else (), pwrites=() if h == 0 else [b_Sf[d]])
                            for h in range(2):
                                k.op("act", lambda e, d=d, h=h: e.activation(out=Sb[d][h][h * 64:(h + 1) * 64, :], in_=Sf[d][h * 64:(h + 1) * 64, :], func=AF.Copy), reads=[b_Sf[d]], writes=[b_Sb[d]] if h == 0 else (), pwrites=() if h == 0 else [b_Sb[d]])
                            lastg = (cc == 3 or cch == NCHl - 1) if d == 0 else (cc == 0)
                            if lastg:
                                ncc = min(4, NCHl - g * 4)
                                for h in range(2):
                                    src = po_t[:].rearrange("p (c h t) -> p c h t", h=2, t=64)[:, 0:ncc, h, :]
                                    dsta = oacc[:, h, g * 256:g * 256 + ncc * 64].rearrange("p (c t) -> p c t", t=64)
                                    if not written[g]:
                                        k.op("act", lambda e, src=src, dsta=dsta: e.activation(out=dsta, in_=src, func=AF.Copy), reads=[bpo], pwrites=[b_oacc])
                                    else:
                                        k.op("dve", lambda e, src=src, dsta=dsta: e.tensor_tensor(out=dsta, in0=src, in1=dsta, op=ALU.add), reads=[bpo, b_oacc], pwrites=[b_oacc])
                                written[g] = True
                    for h in range(2):
                        k.dma("sp", ob_dst[:, hp * 2 + h, :], oacc[:, h, :], reads=[b_oacc], pwrites=[db(ob_dst)], sbuf=b_oacc)
                    for d in range(2):
                        if st_dst is not None:
                            k.dma("sp", st_dst[:, d, hp, :], Sf[d][:], reads=[b_Sf[d]], pwrites=[db(st_dst)], sbuf=b_Sf[d])
                        if pubf is not None:
                            o = (d * HP + hp) * 129
                            sd = sbt(es3, "sd%d" % d, [128, 130], F32); b_sd = Buf("sd%d" % d)
                            k.op("act", lambda e, d=d, sd=sd: e.activation(out=sd[:, 0:128], in_=Sf[d][:], func=AF.Copy), reads=[b_Sf[d]], writes=[b_sd])
                            k.op("dve", lambda e, d=d, sd=sd: e.reduce_sum(out=sd[:, 129:130], in_=tot[d][:], axis=AX.X), reads=[b_pre[d]], pwrites=[b_sd])
                            k.op("act", lambda e, sd=sd: e.activation(out=sd[:, 128:129], in_=sd[:, 129:130], func=AF.Exp, scale=-I16), reads=[b_sd], pwrites=[b_sd])
                            k.dma("sp", pubf[:, o:o + 129], sd[:, 0:129], reads=[b_sd], pwrites=[db(pubf)], sbuf=b_sd)
                    k.barrier()


    selb = sbt(es_glob, "selb", [128, c.CPB, 8], F32); posw = sbt(es_glob, "posw", [128, c.CPB], F32); b_selb = Buf("selb")
    k.dma("sp", selb[:], selb_in, pwrites=[b_selb], sbuf=b_selb)
    k.dma("sp", posw[:], pos_in, pwrites=[b_selb], sbuf=b_selb)

    def gla_final(l, NT, ob_src, rg_src, y_dst, latent):
        CPB = c.CPB
        PF = c.PF
        NCHl = NT // 64
        with ExitStack() as es:
            psc = [pst(es, "psc%d" % i, [128, 512], F32) for i in range(2)]; b_psc = [Buf("psc0"), Buf("psc1")]
            psn2 = [pst(es, "psn2%d" % i, [128, 512], F32) for i in range(2)]; b_psn2 = [Buf("psn20"), Buf("psn21")]
            Sin = sbt(es, "Sin", [128, 2, HP, 2, 128], BF16); b_Sin = Buf("Sin")
            k.op("dve", lambda e: e.memset(Sin[:], 0.0), writes=[b_Sin])
            if latent:
                GF = sbt(es, "GF", [128, 8, PF], F32); b_GF = Buf("GF")
                XJ = sbt(es, "XJ", [128, CPB, PF], F32); b_XJ = Buf("XJ")
                cs = sbt(es, "cs", [128, 2, HP, 128], F32); b_cs = Buf("cs")
                cur = sbt(es, "cur", [128, 128], F32); b_cur = Buf("cur")
                sacc = sbt(es, "sacc", [128, 128], F32); b_sacc = Buf("sacc")
                k.dma("sp", GF[:], gat_f[l].rearrange("(r p) f -> p r f", p=128), reads=[db(gat_f[l])], writes=[b_GF], sbuf=b_GF)
                k.dma("sp", cs[:], cstate[l], reads=[db(cstate[l])], writes=[b_cs], sbuf=b_cs)
                for j in range(CPB):
                    for r in range(8):
                        if r == 0:
                            k.op("dve", lambda e, j=j, r=r: e.tensor_scalar(out=XJ[:, j, :], in0=GF[:, r, :], scalar1=selb[:, j, r:r + 1], scalar2=None, op0=ALU.mult), reads=[b_GF, b_selb], pwrites=[b_XJ])
                        else:
                            k.op("dve", lambda e, j=j, r=r: e.scalar_tensor_tensor(out=XJ[:, j, :], in0=GF[:, r, :], scalar=selb[:, j, r:r + 1], in1=XJ[:, j, :], op0=ALU.mult, op1=ALU.add), reads=[b_GF, b_selb, b_XJ], pwrites=[b_XJ])
                for d in range(2):
                    for hp in range(HP):
                        o = (d * HP + hp) * 129
                        order = list(range(CPB)) if d == 0 else list(range(CPB - 1, -1, -1))
                        k.op("dve", lambda e, d=d, hp=hp: e.tensor_copy(out=cur[:], in_=cs[:, d, hp, :]), reads=[b_cs], writes=[b_cur])
                        for ii, j in enumerate(order):
                            if ii == 0:
                                k.op("dve", lambda e, j=j: e.tensor_scalar(out=sacc[:], in0=cur[:], scalar1=posw[:, j:j + 1], scalar2=None, op0=ALU.mult), reads=[b_cur, b_selb], writes=[b_sacc])
                            else:
                                k.op("dve", lambda e, j=j: e.scalar_tensor_tensor(out=sacc[:], in0=cur[:], scalar=posw[:, j:j + 1], in1=sacc[:], op0=ALU.mult, op1=ALU.add), reads=[b_cur, b_selb, b_sacc], writes=[b_sacc])
                            if ii < CPB - 1:
                                k.op("dve", lambda e, j=j, o=o: e.scalar_tensor_tensor(out=cur[:], in0=cur[:], scalar=XJ[:, j, o + 128:o + 129], in1=XJ[:, j, o:o + 128], op0=ALU.mult, op1=ALU.add), reads=[b_cur, b_XJ], writes=[b_cur])
                        for h in range(2):
                            k.op("act", lambda e, d=d, hp=hp, h=h: e.activation(out=Sin[h * 64:(h + 1) * 64, d, hp, h, :], in_=sacc[h * 64:(h + 1) * 64, :], func=AF.Copy), reads=[b_sacc], pwrites=[b_Sin])
                if dbg:
                    pass
                tt_ = sbt(es, "tt_", [128, 2, HP, NCHl], F32); b_tt = Buf("tt_")
                cum = sbt(es, "cum", [128, 2, HP, NCHl], F32); b_cum = Buf("cum")
                Ec = sbt(es, "Ec", [128, 2, HP, NCHl], F32); b_Ec = Buf("Ec")
                onesc = sbt(es, "onesc", [128, NCHl], F32); b_onesc = Buf("onesc")
                tsum = sbt(es, "tsum", [128, 2], F32); b_tsum = Buf("tsum")
                k.op("dve", lambda e: e.memset(onesc[:], 1.0), writes=[b_onesc])
                k.dma("sp", tt_[:], totd[l], reads=[db(totd[l])], writes=[b_tt], sbuf=b_tt)
                for d in range(2):
                    for hp in range(HP):
                        k.op("dve", lambda e, d=d, hp=hp: e.tensor_tensor_scan(out=cum[:, d, hp, :], data0=onesc[:], data1=tt_[:, d, hp, :], initial=0.0, op0=ALU.mult, op1=ALU.add), reads=[b_tt, b_onesc], pwrites=[b_cum])
                        if d == 0:
                            k.op("dve", lambda e, d=d, hp=hp: e.tensor_tensor(out=cum[:, d, hp, :], in0=cum[:, d, hp, :], in1=tt_[:, d, hp, :], op=ALU.subtract), reads=[b_tt, b_cum], pwrites=[b_cum])
                        else:
                            k.op("dve", lambda e, d=d, hp=hp: e.tensor_scalar(out=cum[:, d, hp, :], in0=cum[:, d, hp, :], scalar1=cum[:, d, hp, NCHl - 1:NCHl], scalar2=-1.0, op0=ALU.subtract, op1=ALU.mult), reads=[b_cum], pwrites=[b_cum])
                k.op("act", lambda e: e.activation(out=Ec[:], in_=cum[:], func=AF.Exp, scale=-I16), reads=[b_cum], writes=[b_Ec])
            qcl = [sbt(es, "qcl%d" % i, [128, 512], BF16) for i in range(4)]; b_qcl = [Buf("qcl%d" % i) for i in range(4)]
            qcc = [sbt(es, "qcc%d" % i, [128, 512], BF16) for i in range(4)]; b_qcc = [Buf("qcc%d" % i) for i in range(4)]
            ot = [sbt(es, "ot%d" % i, [128, 512], F32) for i in range(2)]; b_ot = [Buf("ot0"), Buf("ot1")]
            rgt = [sbt(es, "rgt%d" % i, [128, 512], BF16) for i in range(2)]; b_rgt = [Buf("rgt0"), Buf("rgt1")]
            sq2 = [sbt(es, "sq2%d" % i, [128, 512], BF16) for i in range(2)]; b_sq2 = [Buf("sq20"), Buf("sq21")]
            rs2 = [sbt(es, "rs2%d" % i, [128, 512], F32) for i in range(2)]; b_rs2 = [Buf("rs20"), Buf("rs21")]
            y1 = [sbt(es, "y1%d" % i, [128, 512], F32) for i in range(2)]; b_y1 = [Buf("y10"), Buf("y11")]
            yo = [sbt(es, "yo%d" % i, [128, 512], BF16) for i in range(2)]; b_yo = [Buf("yo0"), Buf("yo1")]
            it = 0
            qi = 0
            TP = min(512, NT)
            for hp in range(HP):
                for t0 in range(0, NT, TP):
                    n = min(TP, NT - t0); ncn = n // 64; c0 = t0 // 64
                    if latent:
                        qq = []
                        for d in range(2):
                            a = qi % 4; qi += 1
                            k.dma("sp", qcl[a][:, 0:n], qcT[l][:, d, hp, t0:t0 + n], reads=[db(qcT[l])], writes=[b_qcl[a]], sbuf=b_qcl[a])
                            eb = Ec[:, d, hp, c0:c0 + ncn].unsqueeze(2).to_broadcast([128, ncn, 64])
                            k.op("dve", lambda e, a=a, eb=eb: e.tensor_tensor(out=qcc[a][:, 0:n].rearrange("p (c t) -> p c t", t=64), in0=qcl[a][:, 0:n].rearrange("p (c t) -> p c t", t=64), in1=eb, op=ALU.mult), reads=[b_qcl[a], b_Ec], writes=[b_qcc[a]])
                            qq.append(a)
                    for h in range(2):
                        head = hp * 2 + h
                        j = it % 2; it += 1
                        k.dma("sp", ot[j][:, 0:n], ob_src[:, head, t0:t0 + n], reads=[db(ob_src)], writes=[b_ot[j]], sbuf=b_ot[j])
                        k.dma("sp", rgt[j][:, 0:n], rg_src[:, head, t0:t0 + n], reads=[db(rg_src)], writes=[b_rgt[j]], sbuf=b_rgt[j])
                        if latent:
                            for d in range(2):
                                a = qq[d]
                                k.op("pe", lambda e, d=d, a=a, h=h, hp=hp, j=j: e.matmul(psc[j][:, 0:n], lhsT=Sin[:, d, hp, h, :], rhs=qcc[a][:, 0:n], start=(d == 0), stop=(d == 1)),
                                     reads=[b_Sin, b_qcc[a]], writes=[b_psc[j]] if d == 0 else (), pwrites=() if d == 0 else [b_psc[j]], inc=(d == 1))
                            k.op("dve", lambda e, j=j: e.tensor_tensor(out=ot[j][:, 0:n], in0=psc[j][:, 0:n], in1=ot[j][:, 0:n], op=ALU.add), reads=[b_psc[j], b_ot[j]], writes=[b_ot[j]])
                        k.op("act", lambda e, j=j: e.activation(out=sq2[j][:, 0:n], in_=ot[j][:, 0:n], func=AF.Square), reads=[b_ot[j]], writes=[b_sq2[j]])
                        k.op("pe", lambda e, j=j: e.matmul(psn2[j][:, 0:n], lhsT=mean128[:], rhs=sq2[j][:, 0:n], start=True, stop=True), reads=[b_sq2[j], b_mean128], writes=[b_psn2[j]])
                        k.op("act", lambda e, j=j: e.activation(out=rs2[j][:, 0:n], in_=psn2[j][:, 0:n], func=AF.Sqrt, bias=EPSC[:, 0:1], scale=1.0), reads=[b_psn2[j], b_EPSC], writes=[b_rs2[j]])
                        k.op("dve", lambda e, j=j: e.reciprocal(out=rs2[j][:, 0:n], in_=rs2[j][:, 0:n]), reads=[b_rs2[j]], writes=[b_rs2[j]])
                        k.op("dve", lambda e, j=j: e.scalar_tensor_tensor(out=y1[j][:, 0:n], in0=ot[j][:, 0:n], scalar=ggc[:, l:l + 1], in1=rs2[j][:, 0:n], op0=ALU.mult, op1=ALU.mult), reads=[b_ot[j], b_rs2[j], b_ggc], writes=[b_y1[j]])
                        k.op("dve", lambda e, j=j: e.tensor_tensor(out=yo[j][:, 0:n], in0=y1[j][:, 0:n], in1=rgt[j][:, 0:n], op=ALU.mult), reads=[b_y1[j], b_rgt[j]], writes=[b_yo[j]])
                        k.dma("sp", y_dst[:, NAH + head, t0:t0 + n], yo[j][:, 0:n], reads=[b_yo[j]], pwrites=[db(y_dst)], sbuf=b_yo[j])
            k.barrier()


    def wout_phase(l, NT, kind, y_src, x_src, x1_dst, h2_dst):
        TP = min(512, NT)
        with ExitStack() as es:
            wo = sbt(es, "wo", [128, KD, D], BF16); b_wo = Buf("wo")
            for kc in range(KD):
                k.dma("pool", wo[:, kc, :], w_out[l][kc * 128:(kc + 1) * 128, :], pwrites=[b_wo], sbuf=b_wo)
            yt = [sbt(es, "yt%d" % i, [128, KD, TP], BF16) for i in range(2)]; b_yt = [Buf("yt0"), Buf("yt1")]
            xt = [sbt(es, "xw%d" % i, [128, KD, TP], F32) for i in range(2)]; b_xt = [Buf("xw0"), Buf("xw1")]
            ps_n = pst(es, "psn", [128, 512], F32); b_ps_n = Buf("psn")
            scr = norm_scratch(es, "nw", ps_n, b_ps_n)
            h2 = [scr[0]] * 2; b_h2 = [scr[1]] * 2
            pw = [pst(es, "pw%d" % i, [128, 512], F32) for i in range(4)]; b_pw = [Buf("pw%d" % i) for i in range(4)]
            GA1 = MODC[:, l, kind, 2, :]
            A2 = MODC[:, l, kind, 3, :]; B2 = MODC[:, l, kind, 4, :]
            pi = 0
            for ti_, t0 in enumerate(range(0, NT, TP)):
                n = min(TP, NT - t0)
                j = ti_ % 2
                k.dma("sp", yt[j][:, :, 0:n], y_src[:, :, t0:t0 + n], reads=[db(y_src)], writes=[b_yt[j]], sbuf=b_yt[j])
                k.dma("sp", xt[j][:, :, 0:n], x_src[:, :, t0:t0 + n], reads=[db(x_src)], writes=[b_xt[j]], sbuf=b_xt[j])
                for dc in range(KD):
                    p = pi % 4; pi += 1
                    for kc in range(KD):
                        k.op("pe", lambda e, kc=kc, dc=dc, p=p, j=j: e.matmul(pw[p][:, 0:n], lhsT=wo[:, kc, dc * 128:(dc + 1) * 128], rhs=yt[j][:, kc, 0:n], start=(kc == 0), stop=(kc == KD - 1)),
                             reads=[b_wo, b_yt[j]], writes=[b_pw[p]] if kc == 0 else (), pwrites=() if kc == 0 else [b_pw[p]], inc=(kc == KD - 1))
                    k.op("dve", lambda e, dc=dc, p=p, j=j: e.scalar_tensor_tensor(out=xt[j][:, dc, 0:n], in0=pw[p][:, 0:n], scalar=GA1[:, dc:dc + 1], in1=xt[j][:, dc, 0:n], op0=ALU.mult, op1=ALU.add),
                         reads=[b_pw[p], b_MODC, b_xt[j]], pwrites=[b_xt[j]])
                k.dma("sp", x1_dst[:, :, t0:t0 + n], xt[j][:, :, 0:n], reads=[b_xt[j]], pwrites=[db(x1_dst)], sbuf=b_xt[j])
                norm_tile("nw", xt[j], b_xt[j], n, A2, B2, b_MODC, h2[j], b_h2[j], scr)
                k.dma("act", h2_dst[:, :, t0:t0 + n], h2[j][:, :, 0:n], reads=[b_h2[j]], pwrites=[db(h2_dst)], sbuf=b_h2[j])
            k.barrier()

    def ffn_phase(l, NT, kind, h2_src, x1_src, x2_dst):
        ST = min(NT, 1024)
        WG = 256
        with ExitStack() as es:
            h2 = sbt(es, "fh2", [128, KD, ST], BF16); b_h2 = Buf("fh2")
            uT = sbt(es, "uT", [128, FC, ST], BF16); b_uT = Buf("uT")
            w1 = [sbt(es, "w1%d" % i, [128, KD, WG], BF16) for i in range(2)]; b_w1 = [Buf("w10"), Buf("w11")]
            FH = FC // 2
            w2 = [sbt(es, "w2%d" % i, [128, FH, 128], BF16) for i in range(2)]; b_w2 = [Buf("w20"), Buf("w21")]
            rl = [sbt(es, "rl%d" % i, [128, 512], F32) for i in range(2)]; b_rl = [Buf("rl0"), Buf("rl1")]
            xr = [sbt(es, "xr%d" % i, [128, 512], F32) for i in range(2)]; b_xr = [Buf("xr0"), Buf("xr1")]
            pf = [pst(es, "pf%d" % i, [128, 512], F32) for i in range(4)]; b_pf = [Buf("pf%d" % i) for i in range(4)]
            pg = [pst(es, "pg%d" % i, [128, 512], F32) for i in range(4)]; b_pg = [Buf("pg%d" % i) for i in range(4)]
            GA2 = MODC[:, l, kind, 5, :]
            w1src = w_ff1[l].rearrange("(kc p) f -> p kc f", p=128)
            w2src = w_ff2[l].rearrange("(fc p) d -> p fc d", p=128)
            cw = dict(a=0, b=0, p=0, g=0, r=0, x=0)
            for s0 in range(0, NT, ST):
                ns = min(ST, NT - s0)
                k.dma("sp", h2[:, :, 0:ns], h2_src[:, :, s0:s0 + ns], reads=[db(h2_src)], writes=[b_h2], sbuf=b_h2)
                for f0 in range(0, c.DFF, WG):
                    wj = cw["a"] % 2; cw["a"] += 1
                    k.dma("pool", w1[wj][:], w1src[:, :, f0:f0 + WG], writes=[b_w1[wj]], sbuf=b_w1[wj])
                    for fc0 in range(0, WG, 128):
                        fc = (f0 + fc0) // 128
                        for n0 in range(0, ns, 512):
                            n = min(512, ns - n0)
                            p = cw["p"] % 4; cw["p"] += 1
                            for kc in range(KD):
                                k.op("pe", lambda e, kc=kc, p=p, wj=wj, fc0=fc0, n0=n0, n=n: e.matmul(pf[p][:, 0:n], lhsT=w1[wj][:, kc, fc0:fc0 + 128], rhs=h2[:, kc, n0:n0 + n], start=(kc == 0), stop=(kc == KD - 1)),
                                     reads=[b_w1[wj], b_h2], writes=[b_pf[p]] if kc == 0 else (), pwrites=() if kc == 0 else [b_pf[p]], inc=(kc == KD - 1))
                            r = cw["r"] % 2; cw["r"] += 1
                            k.op("act", lambda e, p=p, r=r, n=n: e.activation(out=rl[r][:, 0:n], in_=pf[p][:, 0:n], func=AF.Relu), reads=[b_pf[p]], writes=[b_rl[r]])
                            k.op("dve", lambda e, r=r, fc=fc, n0=n0, n=n: e.tensor_tensor(out=uT[:, fc, n0:n0 + n], in0=rl[r][:, 0:n], in1=rl[r][:, 0:n], op=ALU.mult), reads=[b_rl[r]], pwrites=[b_uT])
                for dc in range(KD):
                    wjs = []
                    for fh in range(2):
                        wj = cw["b"] % 2; cw["b"] += 1
                        wjs.append(wj)
                    nts = list(range(0, ns, 512))
                    ps_ = []
                    for n0 in nts:
                        p = cw["g"] % 4; cw["g"] += 1
                        xj = cw["x"] % 2; cw["x"] += 1
                        ps_.append((p, xj))
                        n = min(512, ns - n0)
                        k.dma("sp", xr[xj][:, 0:n], x1_src[:, dc, s0 + n0:s0 + n0 + n], reads=[db(x1_src)], writes=[b_xr[xj]], sbuf=b_xr[xj])
                    for fh in range(2):
                        wj = wjs[fh]
                        k.dma("pool", w2[wj][:], w2src[:, fh * FH:(fh + 1) * FH, dc * 128:(dc + 1) * 128], writes=[b_w2[wj]], sbuf=b_w2[wj])
                        for ni, n0 in enumerate(nts):
                            n = min(512, ns - n0)
                            p, xj = ps_[ni]
                            for f_ in range(FH):
                                fc = fh * FH + f_
                                k.op("pe", lambda e, fc=fc, f_=f_, p=p, wj=wj, n0=n0, n=n: e.matmul(pg[p][:, 0:n], lhsT=w2[wj][:, f_, :], rhs=uT[:, fc, n0:n0 + n], start=(fc == 0), stop=(fc == FC - 1)),
                                     reads=[b_w2[wj], b_uT], writes=[b_pg[p]] if fc == 0 else (), pwrites=() if fc == 0 else [b_pg[p]], inc=(f_ == FH - 1))
                    for ni, n0 in enumerate(nts):
                        n = min(512, ns - n0)
                        p, xj = ps_[ni]
                        k.op("dve", lambda e, p=p, xj=xj, dc=dc, n=n: e.scalar_tensor_tensor(out=xr[xj][:, 0:n], in0=pg[p][:, 0:n], scalar=GA2[:, dc:dc + 1], in1=xr[xj][:, 0:n], op0=ALU.mult, op1=ALU.add),
                             reads=[b_pg[p], b_MODC, b_xr[xj]], writes=[b_xr[xj]])
                        k.dma("sp", x2_dst[:, dc, s0 + n0:s0 + n0 + n], xr[xj][:, 0:n], reads=[b_xr[xj]], pwrites=[db(x2_dst)], sbuf=b_xr[xj])
            k.barrier()

    def final_phase(x_src):
        with ExitStack() as es:
            xt = [sbt(es, "xf%d" % i, [128, KD, 512], F32) for i in range(2)]; b_xt = [Buf("xf0"), Buf("xf1")]
            sq = sbt(es, "fsq", [128, KD, 512], BF16); b_sq = Buf("fsq")
            rstd = sbt(es, "frstd", [128, 512], F32); b_rstd = Buf("frstd")
            ps = pst(es, "fps", [128, 512], F32); b_ps = Buf("fps")
            pt = [pst(es, "fpt%d" % i, [128, 512], F32) for i in range(4)]; b_pt = [Buf("fpt%d" % i) for i in range(4)]
            ost = [sbt(es, "ost%d" % i, [128, D], F32) for i in range(2)]; b_ost = [Buf("ost0"), Buf("ost1")]
            g = 0
            oi = 0
            for ti_, t0 in enumerate(range(0, T, 512)):
                j = ti_ % 2
                k.dma("sp", xt[j][:], x_src[:, :, t0:t0 + 512], reads=[db(x_src)], writes=[b_xt[j]], sbuf=b_xt[j])
                k.op("act", lambda e, j=j: e.activation(out=sq[:], in_=xt[j][:], func=AF.Square), reads=[b_xt[j]], writes=[b_sq])
                for kc in range(KD):
                    k.op("pe", lambda e, kc=kc: e.matmul(ps[:], lhsT=meanD[:], rhs=sq[:, kc, :], start=(kc == 0), stop=(kc == KD - 1)),
                         reads=[b_sq, b_meanD], writes=[b_ps] if kc == 0 else (), pwrites=() if kc == 0 else [b_ps], inc=(kc == KD - 1))
                k.op("act", lambda e: e.activation(out=rstd[:], in_=ps[:], func=AF.Sqrt, bias=EPSC[:, 0:1], scale=1.0), reads=[b_ps, b_EPSC], writes=[b_rstd])
                k.op("dve", lambda e: e.reciprocal(out=rstd[:], in_=rstd[:]), reads=[b_rstd], writes=[b_rstd])
                for kc in range(KD):
                    k.op("dve", lambda e, kc=kc, j=j: e.scalar_tensor_tensor(out=xt[j][:, kc, :], in0=xt[j][:, kc, :], scalar=fgT[:, kc:kc + 1], in1=rstd[:], op0=ALU.mult, op1=ALU.mult),
                         reads=[b_xt[j], b_fgT, b_rstd], pwrites=[b_xt[j]])
                for sub in range(4):
                    oj = oi % 2; oi += 1
                    for k4 in range(0, KD, 4):
                        pj = g % 4; g += 1
                        for q in range(4):
                            k.op("pe", lambda e, q=q, k4=k4, pj=pj, j=j, sub=sub: e.transpose(out=pt[pj][:, q * 128:(q + 1) * 128], in_=xt[j][:, k4 + q, sub * 128:(sub + 1) * 128], identity=ident_f[:]),
                                 reads=[b_xt[j], b_ident_f], writes=[b_pt[pj]] if q == 0 else (), pwrites=() if q == 0 else [b_pt[pj]], inc=(q == 3))
                        if (k4 // 4) % 2 == 0:
                            k.op("act", lambda e, k4=k4, pj=pj, oj=oj: e.activation(out=ost[oj][:, k4 * 128:(k4 + 4) * 128], in_=pt[pj][:], func=AF.Copy), reads=[b_pt[pj]], pwrites=[b_ost[oj]])
                        else:
                            k.op("dve", lambda e, k4=k4, pj=pj, oj=oj: e.tensor_copy(out=ost[oj][:, k4 * 128:(k4 + 4) * 128], in_=pt[pj][:]), reads=[b_pt[pj]], pwrites=[b_ost[oj]])
                    r0 = t0 + sub * 128
                    k.dma("sp", out_d[r0:r0 + 128, :], ost[oj][:], reads=[b_ost[oj]], pwrites=[db(out_d)], sbuf=b_ost[oj])
            k.barrier()

    for l in range(L):
        with_ctx = l < L - 1
        proj_phase(l, xcT[l], M, 1, dict(qa=cqaT[l], ka=ckaT[l], qs=cqsT[l], ks=cksT[l], qg=cqgT[l], kg=ckgT[l], rg=crgT[l], gt=cgtT[l], va=cva[l], vs=cvs[l], vg=cvg[l]), rope=False)
        proj_phase(l, xT[l], T, 0, dict(qa=qaT[l], ka=kaT[l], qs=qsT[l], ks=ksT[l], qg=qgT[l], kg=kgT[l], rg=rgT[l], gt=gtT[l], va=va[l], vs=vs[l], vg=vg[l]), rope=True)
        if STOP == "proj":
            break
        es_halo = ExitStack()
        HALO['P'] = sbt(es_halo, "haloP", [128, c.HB], BF16); HALO['N'] = sbt(es_halo, "haloN", [128, c.HB], BF16)
        publish_halos(l)
        gla_local(l, M, dict(qg=cqgT[l], kg=ckgT[l], gt=cgtT[l], vg=cvg[l]), cobT[l], None, None, cstate[l], None)
        gla_local(l, T, dict(qg=qgT[l], kg=kgT[l], gt=gtT[l], vg=vg[l]), obT[l], qcT[l], totd[l], None, pub_f[l])
        if STOP == "gla1":
            break
        exchange(l)
        na_phase(l, with_ctx)
        swa_phase(l, with_ctx)
        es_halo.close()
        gla_final(l, T, obT[l], rgT[l], yT[l], True)
        if with_ctx:
            gla_final(l, M, cobT[l], crgT[l], cyT[l], False)
        if STOP == "attn":
            break
        wout_phase(l, T, 0, yT[l], xT[l], x1T[l], h2T[l])
        ffn_phase(l, T, 0, h2T[l], x1T[l], xT[l + 1])
        if with_ctx:
            wout_phase(l, M, 1, cyT[l], xcT[l], cx1T[l], ch2T[l])
            ffn_phase(l, M, 1, ch2T[l], cx1T[l], xcT[l + 1])
        if STOP == "l0":
            break
    if STOP is None:
        final_phase(xT[L])
    k.barrier(release=False)
    k.close()
    es_glob.close()

    state = dict(k=k, nc=nc, cfg=c)
    return nc, k, locals()


_CACHE = {}


def kernel(**inputs):
    cfg = Cfg()
    if "nc" not in _CACHE:
        nc, k, _ = build(cfg)
        _CACHE["nc"] = nc
    nc = _CACHE["nc"]
    in_maps = prep_inputs(cfg, inputs)
    res = run_bass_kernel_spmd(nc, in_maps, core_ids=list(range(cfg.NCORE)))
    out = np.empty((cfg.B, cfg.SEQ, cfg.D), np.float32)
    for core in range(cfg.NCORE):
        b, q = core // cfg.CPB, core % cfg.CPB
        out[b, q * cfg.T:(q + 1) * cfg.T] = np.asarray(res.results[core]["out"], np.float32)
    return out
```
